# Optimizing a Trainium2 kernel written in Bass

```python
import jax, jax.numpy as jnp
from jax import lax
import numpy as np

D_MODEL = 2048
BATCH = 2
SEQ = 8192
DEPTH = 1

MLA_HEADS = 8
MLA_Q_RANK = 512
MLA_KV_RANK = 256
MLA_NOPE_DIM = 128
MLA_ROPE_DIM = 64
MLA_QK_DIM = MLA_NOPE_DIM + MLA_ROPE_DIM
MLA_V_DIM = 128
MLA_WIDTH = MLA_HEADS * MLA_V_DIM
ROPE_THETA = 10000.0
Q_BLOCK = 128
NEG_INF = -1e30

GLA_HEADS = 4
GLA_VALUE_DIM = D_MODEL // 2
GLA_KEY_DIM = GLA_VALUE_DIM // 2
GLA_HEAD_K = GLA_KEY_DIM // GLA_HEADS
GLA_HEAD_V = GLA_VALUE_DIM // GLA_HEADS
GLA_GATE_RANK = 16
GLA_GATE_TAU = 16.0
GLA_CHUNK = 64

MIX_WIDTH = MLA_WIDTH + GLA_VALUE_DIM
IN_SPLITS = (MLA_Q_RANK, MLA_KV_RANK, MLA_ROPE_DIM,
             GLA_KEY_DIM, GLA_KEY_DIM, GLA_VALUE_DIM, GLA_GATE_RANK, GLA_VALUE_DIM)
IN_COLS = sum(IN_SPLITS)

D_FF = -(-8 * D_MODEL // (3 * 256)) * 256
N_MOD = 6
RMS_EPS = 1e-6

kernel_name = "hybrid_mla_gla_sandwich_adaln_block"


def rms_norm(x, w, eps=RMS_EPS):
    xf = x.astype(jnp.float32)
    y = xf * lax.rsqrt(jnp.mean(xf * xf, axis=-1, keepdims=True) + eps)
    return (y * w.astype(jnp.float32)).astype(x.dtype)


def split_cols(z, sizes):
    out, start = [], 0
    for s in sizes:
        out.append(z[..., start:start + s])
        start += s
    return out


def rope_cos_sin(positions, dim):
    inv_freq = 1.0 / (ROPE_THETA ** (jnp.arange(0, dim, 2, dtype=jnp.float32) / dim))
    ang = positions.astype(jnp.float32)[..., None] * inv_freq
    return jnp.cos(ang), jnp.sin(ang)


def apply_rope(x, cos, sin):
    half = x.shape[-1] // 2
    x1 = x[..., :half].astype(jnp.float32)
    x2 = x[..., half:].astype(jnp.float32)
    out = jnp.concatenate([x1 * cos - x2 * sin, x2 * cos + x1 * sin], axis=-1)
    return out.astype(x.dtype)


def mla_mixer(c_q, c_kv, k_rope, positions, g_q, w_uq, g_kv, w_uk, w_uv):
    B, S, _ = c_q.shape
    H = MLA_HEADS
    q = (rms_norm(c_q, g_q) @ w_uq).reshape(B, S, H, MLA_QK_DIM)
    c_kv = rms_norm(c_kv, g_kv)
    k_nope = (c_kv @ w_uk).reshape(B, S, H, MLA_NOPE_DIM)
    v = (c_kv @ w_uv).reshape(B, S, H, MLA_V_DIM)
    cos, sin = rope_cos_sin(positions, MLA_ROPE_DIM)
    q_rope = apply_rope(q[..., MLA_NOPE_DIM:], cos[:, :, None, :], sin[:, :, None, :])
    k_rope = apply_rope(k_rope, cos, sin)
    q = jnp.concatenate([q[..., :MLA_NOPE_DIM], q_rope], axis=-1) * (MLA_QK_DIM ** -0.5)
    k = jnp.concatenate(
        [k_nope, jnp.broadcast_to(k_rope[:, :, None, :], (B, S, H, MLA_ROPE_DIM))], axis=-1)
    q = q.transpose(0, 2, 1, 3)
    k = k.transpose(0, 2, 1, 3)
    v = v.transpose(0, 2, 1, 3)
    nb = S // Q_BLOCK
    q_blocks = q.reshape(B, H, nb, Q_BLOCK, MLA_QK_DIM).transpose(2, 0, 1, 3, 4)
    k_pos = jnp.arange(S)

    def attend(args):
        q_blk, blk = args
        q_pos = blk * Q_BLOCK + jnp.arange(Q_BLOCK)
        s = jnp.einsum('bhqd,bhkd->bhqk', q_blk, k).astype(jnp.float32)
        s = jnp.where(k_pos[None, :] <= q_pos[:, None], s, NEG_INF)
        p = jax.nn.softmax(s, axis=-1).astype(v.dtype)
        return jnp.einsum('bhqk,bhkd->bhqd', p, v)

    o = lax.map(attend, (q_blocks, jnp.arange(nb)))
    return o.transpose(1, 0, 3, 2, 4).reshape(B, S, MLA_WIDTH)


def gla_mixer(q, k, v, a_lr, r, w_gate_up, b_gate, g_gla):
    B, S, _ = q.shape
    H, dk, dv, C = GLA_HEADS, GLA_HEAD_K, GLA_HEAD_V, GLA_CHUNK
    N = S // C
    f32 = jnp.float32
    log_a = jax.nn.log_sigmoid((a_lr @ w_gate_up + b_gate).astype(f32)) / GLA_GATE_TAU

    def chunked(t, d):
        return t.reshape(B, N, C, H, d).transpose(0, 3, 1, 2, 4).astype(f32)

    qc = chunked(q, dk) * (dk ** -0.5)
    kc = chunked(k, dk)
    vc = chunked(v, dv)
    bc = jnp.cumsum(chunked(log_a, dk), axis=3)
    b_last = bc[:, :, :, -1:, :]
    q_dec = qc * jnp.exp(bc)
    k_inv = kc * jnp.exp(-bc)
    k_dec = kc * jnp.exp(b_last - bc)
    causal = jnp.tril(jnp.ones((C, C), dtype=bool))
    attn = jnp.einsum('bhncd,bhnjd->bhncj', q_dec, k_inv)
    attn = jnp.where(causal, attn, 0.0)
    o_intra = jnp.einsum('bhncj,bhnjv->bhncv', attn, vc)
    d_state = jnp.einsum('bhncd,bhncv->bhndv', k_dec, vc)
    decay = jnp.exp(b_last[:, :, :, 0, :])

    def step(state, inp):
        q_n, ds_n, decay_n = inp
        o_n = jnp.einsum('bhcd,bhdv->bhcv', q_n, state)
        state = decay_n[..., None] * state + ds_n
        return state, o_n

    init = jnp.zeros((B, H, dk, dv), f32)
    _, o_inter = lax.scan(step, init, (jnp.moveaxis(q_dec, 2, 0),
                                       jnp.moveaxis(d_state, 2, 0),
                                       jnp.moveaxis(decay, 2, 0)))
    o = o_intra + jnp.moveaxis(o_inter, 0, 2)
    o = o.transpose(0, 2, 3, 1, 4).reshape(B, S, H, dv)
    o = rms_norm(o, g_gla) * jax.nn.silu(r.reshape(B, S, H, dv).astype(f32))
    return o.reshape(B, S, GLA_VALUE_DIM).astype(q.dtype)


def setup_inputs(seed: int = 0) -> dict:
    key = jax.random.key(seed)
    ks = jax.random.split(key, 24)
    f32 = jnp.float32
    L = DEPTH

    def nrm(k, shape, scale):
        return jax.random.normal(k, shape, f32) * scale

    def gain(k, shape):
        return 1.0 + 0.02 * jax.random.normal(k, shape, f32)

    offsets = jax.random.randint(ks[2], (BATCH, 1), 0, 4096, dtype=jnp.int32)
    positions = offsets + jnp.arange(SEQ, dtype=jnp.int32)[None, :]
    return {
        'x': nrm(ks[0], (BATCH, SEQ, D_MODEL), 1.0),
        'c': nrm(ks[1], (BATCH, D_MODEL), 1.0),
        'positions': positions,
        'w_ada': nrm(ks[3], (L, D_MODEL, N_MOD * D_MODEL), D_MODEL ** -0.5),
        'b_ada': nrm(ks[4], (L, N_MOD * D_MODEL), 0.01),
        'g_pre_mix': gain(ks[5], (L, D_MODEL)),
        'g_post_mix': gain(ks[6], (L, D_MODEL)),
        'w_in': nrm(ks[7], (L, D_MODEL, IN_COLS), D_MODEL ** -0.5),
        'g_q': gain(ks[8], (L, MLA_Q_RANK)),
        'w_uq': nrm(ks[9], (L, MLA_Q_RANK, MLA_HEADS * MLA_QK_DIM), MLA_Q_RANK ** -0.5),
        'g_kv': gain(ks[10], (L, MLA_KV_RANK)),
        'w_uk': nrm(ks[11], (L, MLA_KV_RANK, MLA_HEADS * MLA_NOPE_DIM), MLA_KV_RANK ** -0.5),
        'w_uv': nrm(ks[12], (L, MLA_KV_RANK, MLA_HEADS * MLA_V_DIM), MLA_KV_RANK ** -0.5),
        'w_gate_up': nrm(ks[13], (L, GLA_GATE_RANK, GLA_KEY_DIM), GLA_GATE_RANK ** -0.5),
        'b_gate': nrm(ks[14], (L, GLA_KEY_DIM), 0.1),
        'g_gla': gain(ks[15], (L, GLA_HEAD_V)),
        'w_out': nrm(ks[16], (L, MIX_WIDTH, D_MODEL), MIX_WIDTH ** -0.5),
        'g_pre_ffn': gain(ks[17], (L, D_MODEL)),
        'g_post_ffn': gain(ks[18], (L, D_MODEL)),
        'w_ffn_gate': nrm(ks[19], (L, D_MODEL, D_FF), D_MODEL ** -0.5),
        'w_ffn_up': nrm(ks[20], (L, D_MODEL, D_FF), D_MODEL ** -0.5),
        'w_ffn_down': nrm(ks[21], (L, D_FF, D_MODEL), D_FF ** -0.5),
    }


def reference(x, c, positions, w_ada, b_ada, g_pre_mix, g_post_mix, w_in, g_q, w_uq,
              g_kv, w_uk, w_uv, w_gate_up, b_gate, g_gla, w_out, g_pre_ffn, g_post_ffn,
              w_ffn_gate, w_ffn_up, w_ffn_down):
    for l in range(DEPTH):
        ada = jax.nn.silu(c) @ w_ada[l] + b_ada[l]
        shift_m, scale_m, gate_m, shift_f, scale_f, gate_f = [
            t[:, None, :] for t in jnp.split(ada, N_MOD, axis=-1)]

        h = rms_norm(x, g_pre_mix[l]) * (1.0 + scale_m) + shift_m
        z = h @ w_in[l]
        c_q, c_kv, k_rope, q_g, k_g, v_g, a_g, r_g = split_cols(z, IN_SPLITS)
        o_mla = mla_mixer(c_q, c_kv, k_rope, positions, g_q[l], w_uq[l],
                          g_kv[l], w_uk[l], w_uv[l])
        o_gla = gla_mixer(q_g, k_g, v_g, a_g, r_g, w_gate_up[l], b_gate[l], g_gla[l])
        o = jnp.concatenate([o_mla, o_gla], axis=-1) @ w_out[l]
        x = x + gate_m * rms_norm(o, g_post_mix[l])

        h = rms_norm(x, g_pre_ffn[l]) * (1.0 + scale_f) + shift_f
        f = (jax.nn.silu(h @ w_ffn_gate[l]) * (h @ w_ffn_up[l])) @ w_ffn_down[l]
        x = x + gate_f * rms_norm(f, g_post_ffn[l])
    return x
```

```python
import numpy as np
import ml_dtypes
from contextlib import ExitStack
import concourse.bass as bass
import concourse.mybir as mybir
from concourse.bass_utils import run_bass_kernel_spmd

F32 = mybir.dt.float32
BF16 = mybir.dt.bfloat16
I32 = mybir.dt.int32
AF = mybir.ActivationFunctionType
ALU = mybir.AluOpType

D = 2048
NT = 8192
NOWN = 2048
NPRE = 6144
NB = 16
NBP = 12
EPS = 1e-6
IN_COLS = 3920
C_Q, C_KV, C_KR, C_QG, C_KG, C_VG, C_AG, C_RG = 0, 512, 768, 832, 1344, 1856, 2880, 2896
D_FF = 5632
NEG = -30000.0
TWO_PI = 2.0 * np.pi
CW1 = 6.28125
CW2 = float(TWO_PI - 6.28125)


class Buf:
    __slots__ = ("w", "r", "name", "psum")

    def __init__(self, name="", psum=False):
        self.w = None
        self.r = {}
        self.name = name
        self.psum = psum


class DSem:
    __slots__ = ("sem", "cnt")

    def __init__(self, sem):
        self.sem = sem
        self.cnt = 0


class Queue:
    def __init__(self, eng, sem, is_pe=False, name=""):
        self.eng = eng
        self.sem = sem
        self.cnt = 0
        self.seen = {}
        self.own_seen = 0
        self.is_pe = is_pe
        self.dsems = []
        self.dk = 0
        self.name = name

    def wait(self, tok):
        s, v = tok
        key = id(s)
        if self.seen.get(key, 0) >= v:
            return
        self.eng.wait_ge(s, v)
        self.seen[key] = v

    def own_wait(self, v):
        if self.own_seen >= v:
            return
        self.eng.wait_ge(self.sem, v)
        self.own_seen = v


class Builder:
    def __init__(self):
        self.nc = bass.Bass("TRN2", target_bir_lowering=False)
        nc = self.nc
        self.es = ExitStack()
        self.PE = Queue(nc.tensor, self._sem("q_pe"), is_pe=True, name="pe")
        self.ACT = Queue(nc.scalar, self._sem("q_act"), name="act")
        self.DVE = Queue(nc.vector, self._sem("q_dve"), name="dve")
        self.POOL = Queue(nc.gpsimd, self._sem("q_pool"), name="pool")
        self.SP = Queue(nc.sync, self._sem("q_sp"), name="sp")
        self.queues = [self.PE, self.ACT, self.DVE, self.POOL, self.SP]
        for q, n in ((self.SP, 8), (self.POOL, 8), (self.ACT, 4)):
            q.dsems = [DSem(self._sem(f"d_{q.name}{i}")) for i in range(n)]
        self.out_toks = []
        self.n_inst = 0

    def _sem(self, name):
        return self.es.enter_context(self.nc.semaphore(name))

    def pre(self, q, reads=(), writes=()):
        for b in reads:
            t = b.w
            if t is not None:
                if t[0] is q.sem:
                    if not q.is_pe:
                        q.own_wait(t[1])
                else:
                    q.wait(t)
        for b in writes:
            t = b.w
            if t is not None:
                if t[0] is q.sem:
                    if not q.is_pe:
                        q.own_wait(t[1])
                else:
                    q.wait(t)
            for (s, v) in b.r.values():
                if s is q.sem:
                    if not q.is_pe:
                        q.own_wait(v)
                else:
                    q.wait((s, v))

    def post(self, q, ins, reads=(), writes=()):
        q.cnt += 1
        ins.then_inc(q.sem, 1)
        tok = (q.sem, q.cnt)
        self._reg(tok, reads, writes)
        self.n_inst += 1

    def _reg(self, tok, reads, writes):
        for b in reads:
            if b.psum:
                b.w = tok
            else:
                b.r[id(tok[0])] = tok
        for b in writes:
            b.w = tok
            b.r = {}

    def op(self, q, fn, reads=(), writes=()):
        self.pre(q, reads, writes)
        ins = fn()
        self.post(q, ins, reads, writes)
        return ins

    def dma(self, q, out, in_, reads=(), writes=(), is_output=False, **kw):
        self.pre(q, reads, writes)
        ds = q.dsems[q.dk % len(q.dsems)]
        q.dk += 1
        if ds.cnt > 0:
            q.wait((ds.sem, ds.cnt * 16))
        ins = q.eng.dma_start(out=out, in_=in_, **kw)
        ds.cnt += 1
        ins.then_inc(ds.sem, 16)
        tok = (ds.sem, ds.cnt * 16)
        self._reg(tok, reads, writes)
        if is_output:
            self.out_toks.append(tok)
        self.n_inst += 1
        return tok

    def barrier(self):
        toks = []
        for q in self.queues:
            if q.cnt > 0:
                toks.append((q.sem, q.cnt))
            for ds in q.dsems:
                if ds.cnt > 0:
                    toks.append((ds.sem, ds.cnt * 16))
        for q in self.queues:
            for t in toks:
                if t[0] is q.sem:
                    continue
                q.wait(t)

    def finish(self):
        self.barrier()


class Prog(Builder):
    def __init__(self, debug=None, phases=("P0", "A1", "G", "ATT", "OUT", "FFN")):
        super().__init__()
        self.debug = debug or set()
        self.phases = tuple(phases)
        self.in_names = []
        nc = self.nc
        dt = nc.dram_tensor

        def din(name, shape, dtype=F32, ph=None):
            if ph is not None and not (set(ph) & set(self.phases)):
                return None
            self.in_names.append(name)
            return dt(name, list(shape), dtype, kind="ExternalInput").ap()

        def dscr(name, shape, dtype):
            kind = "ExternalOutput" if name in self.debug else "Internal"
            return dt(name, list(shape), dtype, kind=kind).ap()

        self.xall = din("xall", [NT, D])
        self.posall = din("posall", [1, NT], I32)
        self.cvec = din("cvec", [128, 16])
        self.slotmask = din("slotmask", [128, 8])
        self.w_ada = din("w_ada", [D, 6 * D], ph=("P0",))
        self.b_ada_in = din("b_ada", [1, 6 * D])
        self.g_pre_mixT = din("g_pre_mixT", [128, 16])
        self.g_post_mix = din("g_post_mix", [1, D])
        self.w_in = din("w_in", [D, IN_COLS])
        self.g_qT = din("g_qT", [128, 4])
        self.w_uq = din("w_uq", [512, 1536], ph=("ATT",))
        self.g_kvT = din("g_kvT", [128, 2])
        self.w_uk = din("w_uk", [256, 1024], ph=("ATT",))
        self.w_uv = din("w_uv", [256, 1024], ph=("ATT",))
        self.w_gate_up = din("w_gate_up", [16, 512])
        self.b_gateT = din("b_gateT", [128, 4])
        self.g_glaT = din("g_glaT", [128, 2])
        self.w_out = din("w_out", [D, D], ph=("OUT",))
        self.g_pre_ffnT = din("g_pre_ffnT", [128, 16])
        self.g_post_ffn = din("g_post_ffn", [1, D])
        self.g_post_ffnT = din("g_post_ffnT", [128, 16])
        self.w_ffn_gate = din("w_ffn_gate", [D, D_FF], ph=("FFN",))
        self.w_ffn_up = din("w_ffn_up", [D, D_FF], ph=("FFN",))
        self.w_ffn_down = din("w_ffn_down", [D_FF, D], ph=("FFN",))
        self.c_ident_bf = din("c_ident_bf", [128, 128], BF16)
        self.c_ident_f = din("c_ident_f", [128, 128])
        self.c_misc = din("c_misc", [128, 8])
        self.c_keep = din("c_keep", [128, 512])
        self.c_tri_bf = din("c_tri_bf", [128, 128], BF16)
        self.c_mask128 = din("c_mask128", [128, 128])
        self.out = dt("out", [NOWN, D], F32, kind="ExternalOutput").ap()
        self.HT = dscr("HT", [NB, 128, 16, 512], BF16)
        self.CKVN = dscr("CKVN", [2, 128, NT], BF16)
        self.KROPE = dscr("KROPE", [64, NT], BF16)
        self.CQN = dscr("CQN", [4, 128, NOWN], BF16)
        self.MIXT = dscr("MIXT", [16, 128, NOWN], BF16)
        self.ADAT = dscr("ADAT", [128, 96], F32)
        self.ROPE = dscr("ROPE", [2, 64, NT], F32)
        self.GBC = dscr("GBC", [2, 128, D], F32)
        self.X1 = dscr("X1", [NOWN, D], F32)
        self.H2T = dscr("H2T", [4, 128, 16, 512], BF16)
        self.WG = dscr("WG", [D, D_FF], BF16)
        self.WU = dscr("WU", [D, D_FF], BF16)
        self.WD = dscr("WD", [D_FF, D], BF16)

    def rsqrt(self, ap, buf):
        nc = self.nc
        self.op(self.ACT, lambda: nc.scalar.activation(out=ap, in_=ap, func=AF.Ln), reads=[buf], writes=[buf])
        self.op(self.ACT, lambda: nc.scalar.activation(out=ap, in_=ap, func=AF.Exp, scale=-0.5), reads=[buf], writes=[buf])

    def sb(self, es, name, shape, dtype):
        return es.enter_context(self.nc.sbuf_tensor(name, list(shape), dtype))

    def ps(self, es, name, shape, dtype):
        return es.enter_context(self.nc.psum_tensor(name, list(shape), dtype))

    def bank(self, es, name, dtype=F32):
        n = 512 if dtype == F32 else 1024
        return es.enter_context(self.nc.psum_tensor(name, [128, n], dtype))

    def build(self):
        nc = self.nc
        phases = self.phases
        PE, ACT, DVE, POOL, SP = self.PE, self.ACT, self.DVE, self.POOL, self.SP
        es0 = self.es
        self.ident_bf = self.sb(es0, "ident_bf", [128, 128], BF16)
        self.ident_f = self.sb(es0, "ident_f", [128, 128], F32)
        self.ones_bf = self.sb(es0, "ones_bf", [128, 128], BF16)
        self.ones_f = self.sb(es0, "ones_f", [128, 128], F32)
        self.misc = self.sb(es0, "misc", [128, 8], F32)
        self.smask = self.sb(es0, "smask", [128, 8], F32)
        self.adaT = self.sb(es0, "adaT", [128, 96], F32)
        self.Am = self.sb(es0, "Am", [128, 16], F32)
        self.Af = self.sb(es0, "Af", [128, 16], F32)
        self.gpreT = self.sb(es0, "gpreT", [128, 32], F32)
        self.gqT = self.sb(es0, "gqT", [128, 4], F32)
        self.gkvT = self.sb(es0, "gkvT", [128, 2], F32)
        self.gfT = self.sb(es0, "gfT", [128, 16], F32)
        self.negb = self.sb(es0, "negb", [128, 4], F32)
        self.gglaT = self.sb(es0, "gglaT", [128, 2], F32)
        self.b_gbc = Buf("gbc_dram")
        self.b_const = Buf("const")
        self.b_ada = Buf("adaT")
        self.b_gm = Buf("gm")
        self.b_gf = Buf("gf")
        self.b_AB = Buf("AB")

        def ld(q, dst, src, b):
            self.dma(q, dst, src, writes=[b])

        bc = self.b_const
        ld(SP, self.ident_bf[:], self.c_ident_bf[:, :], bc)
        ld(SP, self.ident_f[:], self.c_ident_f[:, :], bc)
        ld(SP, self.misc[:], self.c_misc[:, :], bc)
        ld(SP, self.smask[:], self.slotmask[:, :], bc)
        self.dma(SP, self.gpreT[:, 0:16], self.g_pre_mixT[:, :], writes=[bc])
        self.dma(SP, self.gpreT[:, 16:32], self.g_pre_ffnT[:, :], writes=[bc])
        self.dma(SP, self.gqT[:], self.g_qT[:, :], writes=[bc])
        self.dma(SP, self.gkvT[:], self.g_kvT[:, :], writes=[bc])
        self.dma(SP, self.gfT[:], self.g_post_ffnT[:, :], writes=[bc])
        self.dma(SP, self.negb[:], self.b_gateT[:, :], writes=[bc])
        self.dma(SP, self.gglaT[:], self.g_glaT[:, :], writes=[bc])
        self.op(DVE, lambda: nc.vector.tensor_scalar(out=self.negb[:], in0=self.negb[:], scalar1=-1.0, scalar2=None, op0=ALU.mult), reads=[bc], writes=[bc])
        self.op(DVE, lambda: nc.vector.memset(self.ones_f[:], 1.0), writes=[bc])
        self.op(DVE, lambda: nc.vector.memset(self.ones_bf[:], 1.0), writes=[bc])

        if "P0" in phases:
            self.phase_p0()
        if "A1" in phases:
            self.barrier()
            self.phase_a1()
        if "G" in phases:
            self.barrier()
            self.phase_g()
        if "ATT" in phases:
            self.barrier()
            self.phase_att()
        if "OUT" in phases:
            self.barrier()
            self.phase_out()
        if "FFN" in phases:
            self.barrier()
            self.phase_ffn()
        self.finish()
        return nc

    def gen_rope(self, es):
        nc = self.nc
        PE, ACT, DVE, POOL, SP = self.PE, self.ACT, self.DVE, self.POOL, self.SP
        posi = [self.sb(es, f"r_posi{i}", [64, 512], I32) for i in range(2)]
        posf = self.sb(es, "r_posf", [64, 512], F32)
        ang = self.sb(es, "r_ang", [64, 512], F32)
        rt = self.sb(es, "r_rt", [64, 2, 512], F32)
        rti = self.sb(es, "r_rti", [64, 2, 512], I32)
        rope = [self.sb(es, f"r_rope{i}", [64, 2, 512], F32) for i in range(2)]
        b_posi = [Buf(), Buf()]
        b_posf, b_ang, b_rt, b_rti = Buf(), Buf(), Buf(), Buf()
        b_rope = [Buf(), Buf()]
        b_dram = Buf()

        def rope_block(tb):
            t0 = tb * 512
            rb = tb % 2
            pi_ = tb % 2
            self.dma(SP, posi[pi_][:], self.posall[0:1, t0:t0 + 512].to_broadcast([64, 512]), writes=[b_posi[pi_]])
            self.op(DVE, lambda: nc.vector.tensor_copy(out=posf[:], in_=posi[pi_][:]), reads=[b_posi[pi_]], writes=[b_posf])
            self.op(DVE, lambda: nc.vector.tensor_scalar(out=ang[:], in0=posf[:], scalar1=self.misc[0:64, 0:1], scalar2=None, op0=ALU.mult),
                    reads=[b_posf, self.b_const], writes=[b_ang])
            self.op(DVE, lambda: nc.vector.tensor_scalar(out=rt[:, 0, :], in0=ang[:], scalar1=float(1.0 / TWO_PI), scalar2=0.25, op0=ALU.mult, op1=ALU.add),
                    reads=[b_ang], writes=[b_rt])
            self.op(DVE, lambda: nc.vector.tensor_scalar(out=rt[:, 1, :], in0=ang[:], scalar1=float(1.0 / TWO_PI), scalar2=None, op0=ALU.mult),
                    reads=[b_ang], writes=[b_rt])
            self.op(DVE, lambda: nc.vector.tensor_copy(out=rti[:], in_=rt[:]), reads=[b_rt], writes=[b_rti])
            self.op(DVE, lambda: nc.vector.tensor_copy(out=rt[:], in_=rti[:]), reads=[b_rti], writes=[b_rt])
            for k in range(2):
                self.op(DVE, lambda: nc.vector.scalar_tensor_tensor(out=rope[rb][:, k, :], in0=rt[:, k, :], scalar=-CW1, in1=ang[:], op0=ALU.mult, op1=ALU.add),
                        reads=[b_rt, b_ang], writes=[b_rope[rb]])
                self.op(DVE, lambda: nc.vector.scalar_tensor_tensor(out=rope[rb][:, k, :], in0=rt[:, k, :], scalar=-CW2, in1=rope[rb][:, k, :], op0=ALU.mult, op1=ALU.add),
                        reads=[b_rt, b_rope[rb]], writes=[b_rope[rb]])
            self.op(DVE, lambda: nc.vector.tensor_scalar(out=rope[rb][:, 0, :], in0=rope[rb][:, 0, :], scalar1=float(np.pi / 2), scalar2=None, op0=ALU.add),
                    reads=[b_rope[rb]], writes=[b_rope[rb]])
            self.op(DVE, lambda: nc.vector.tensor_scalar(out=rope[rb][:], in0=rope[rb][:], scalar1=float(-np.pi), scalar2=float(np.pi), op0=ALU.max, op1=ALU.min),
                    reads=[b_rope[rb]], writes=[b_rope[rb]])
            self.op(ACT, lambda: nc.scalar.activation(out=rope[rb][:, 0, :], in_=rope[rb][:, 0, :], func=AF.Sin),
                    reads=[b_rope[rb]], writes=[b_rope[rb]])
            self.op(ACT, lambda: nc.scalar.activation(out=rope[rb][:, 1, :], in_=rope[rb][:, 1, :], func=AF.Sin, scale=self.misc[0:64, 1:2]),
                    reads=[b_rope[rb], self.b_const], writes=[b_rope[rb]])
            self.dma(POOL, self.ROPE[:, :, t0:t0 + 512].rearrange("k p n -> p k n"), rope[rb][:], reads=[b_rope[rb]], writes=[b_dram],
                     is_output=("ROPE" in self.debug))

        return rope_block


    def phase_p0(self):
        nc = self.nc
        PE, ACT, DVE, POOL, SP = self.PE, self.ACT, self.DVE, self.POOL, self.SP
        with ExitStack() as es:
            cv = self.sb(es, "p0_cv", [128, 16], F32)
            scb = self.sb(es, "p0_scb", [128, 16], BF16)
            wa = [self.sb(es, f"p0_wa{i}", [128, 16, 512], BF16) for i in range(2)]
            brow = [self.sb(es, f"p0_brow{i}", [1, 512], F32) for i in range(2)]
            row = [self.sb(es, f"p0_row{i}", [1, 512], F32) for i in range(2)]
            gpost = self.sb(es, "p0_gpost", [1, 2, D], F32)
            self.gm_bc = self.sb(es, "gm_bc", [128, D], F32)
            self.gf_bc = self.sb(es, "gf_bc", [128, D], F32)
            row2 = [self.sb(es, f"p0_row2{i}", [1, 512], F32) for i in range(2)]
            b_row2 = [Buf(), Buf()]
            prow_b = [self.bank(es, f"p0_prow{i}") for i in range(2)]
            pcol_b = [self.bank(es, f"p0_pcol{i}") for i in range(2)]
            pbc = [self.bank(es, f"p0_pbc{i}") for i in range(2)]
            prow = [t[0:1, :] for t in prow_b]
            pcol = [t[:, 0:4] for t in pcol_b]
            b_cv, b_scb, b_gpost = Buf(), Buf(), Buf()
            b_wa = [Buf(), Buf()]
            b_brow = [Buf(), Buf()]
            b_row = [Buf(), Buf()]
            b_prow = [Buf(psum=True), Buf(psum=True)]
            b_pcol = [Buf(psum=True), Buf(psum=True)]
            b_pbc = [Buf(psum=True), Buf(psum=True)]
            self.dma(SP, cv[:], self.cvec[:, :], writes=[b_cv])
            self.dma(SP, gpost[:, 0, :], self.g_post_mix[0:1, :], writes=[b_gpost])
            self.dma(SP, gpost[:, 1, :], self.g_post_ffn[0:1, :], writes=[b_gpost])
            self.op(ACT, lambda: nc.scalar.activation(out=scb[:], in_=cv[:], func=AF.Silu), reads=[b_cv], writes=[b_scb])
            rope_block = self.gen_rope(es)
            wsrc = self.w_ada.rearrange("(kc p) n -> p kc n", p=128)
            for nb in range(24):
                if nb < NB:
                    rope_block(nb)
                i = nb % 2
                self.dma(POOL, wa[i][:], wsrc[:, :, nb * 512:(nb + 1) * 512], writes=[b_wa[i]])
                self.dma(SP, brow[i][:], self.b_ada_in[0:1, nb * 512:(nb + 1) * 512], writes=[b_brow[i]])
                self.pre(PE, reads=[b_scb, b_wa[i]], writes=[b_prow[i]])
                for kc in range(16):
                    ins = nc.tensor.matmul(prow[i], lhsT=scb[:, kc:kc + 1], rhs=wa[i][:, kc, :], start=(kc == 0), stop=(kc == 15))
                self.post(PE, ins, reads=[b_scb, b_wa[i]], writes=[b_prow[i]])
                self.op(DVE, lambda: nc.vector.tensor_tensor(out=row[i][:], in0=prow[i], in1=brow[i][:], op=ALU.add),
                        reads=[b_prow[i], b_brow[i]], writes=[b_row[i]])
                self.pre(PE, reads=[b_row[i], self.b_const], writes=[b_pcol[i]])
                for j in range(4):
                    ins = nc.tensor.matmul(pcol[i][:, j:j + 1], lhsT=row[i][0:1, j * 128:(j + 1) * 128], rhs=self.ones_f[0:1, 0:1], start=True, stop=True)
                self.post(PE, ins, reads=[b_row[i], self.b_const], writes=[b_pcol[i]])
                self.op(DVE, lambda: nc.vector.tensor_copy(out=self.adaT[:, nb * 4:nb * 4 + 4], in_=pcol[i]),
                        reads=[b_pcol[i]], writes=[self.b_ada])
                g = nb // 4
                if g in (2, 5):
                    dst, bb, gi = (self.gm_bc, self.b_gm, 0) if g == 2 else (self.gf_bc, self.b_gf, 1)
                    c0 = (nb % 4) * 512
                    self.op(DVE, lambda: nc.vector.tensor_tensor(out=row2[i][:], in0=row[i][:], in1=gpost[0:1, gi, c0:c0 + 512], op=ALU.mult),
                            reads=[b_row[i], b_gpost], writes=[b_row2[i]])
                    self.op(PE, lambda: nc.tensor.matmul(pbc[i][:], lhsT=self.ones_f[0:1, :], rhs=row2[i][:], start=True, stop=True),
                            reads=[b_row2[i], self.b_const], writes=[b_pbc[i]])
                    self.op(DVE, lambda: nc.vector.tensor_copy(out=dst[:, c0:c0 + 512], in_=pbc[i][:]),
                            reads=[b_pbc[i]], writes=[bb])
            self.op(DVE, lambda: nc.vector.scalar_tensor_tensor(out=self.Am[:], in0=self.adaT[:, 16:32], scalar=1.0, in1=self.gpreT[:, 0:16], op0=ALU.add, op1=ALU.mult),
                    reads=[self.b_ada, self.b_const], writes=[self.b_AB])
            self.op(DVE, lambda: nc.vector.scalar_tensor_tensor(out=self.Af[:], in0=self.adaT[:, 64:80], scalar=1.0, in1=self.gpreT[:, 16:32], op0=ALU.add, op1=ALU.mult),
                    reads=[self.b_ada, self.b_const], writes=[self.b_AB])
            self.op(DVE, lambda: nc.vector.tensor_tensor(out=self.gfT[:], in0=self.gfT[:], in1=self.adaT[:, 80:96], op=ALU.mult),
                    reads=[self.b_ada, self.b_const], writes=[self.b_AB])
            self.dma(POOL, self.GBC[0, :, :], self.gm_bc[:], reads=[self.b_gm], writes=[self.b_gbc])
            self.dma(POOL, self.GBC[1, :, :], self.gf_bc[:], reads=[self.b_gf], writes=[self.b_gbc])
            if "ADAT" in self.debug:
                self.dma(POOL, self.ADAT[:, :], self.adaT[:], reads=[self.b_ada], is_output=True)
            self.barrier()

    def phase_a1(self):
        nc = self.nc
        PE, ACT, DVE, POOL, SP = self.PE, self.ACT, self.DVE, self.POOL, self.SP
        NCOL = 512 + 256 + 128
        with ExitStack() as es:
            win = self.sb(es, "a1_win", [128, 16, NCOL], BF16)
            xt = [self.sb(es, f"a1_xt{i}", [128, D], F32) for i in range(4)]
            sq = self.sb(es, "a1_sq", [128, D], BF16)
            ss = self.sb(es, "a1_ss", [128, 8], F32)
            rs = self.sb(es, "a1_rs", [128, 8], F32)
            xn = [self.sb(es, f"a1_xn{i}", [128, 4, D], BF16) for i in range(2)]
            hT = [self.sb(es, f"a1_hT{i}", [128, 16, 512], BF16) for i in range(2)]
            rope = [self.sb(es, f"a1_rope{i}", [64, 2, 512], F32) for i in range(3)]
            raw = self.sb(es, "a1_raw", [128, 4, 512], F32)
            sqz = self.sb(es, "a1_sqz", [128, 4, 512], BF16)
            rbc = self.sb(es, "a1_rbc", [128, 512], F32)
            stg = [self.sb(es, f"a1_stg{i}", [128, 4, 512], BF16) for i in range(2)]
            kt1 = self.sb(es, "a1_kt1", [64, 512], F32)
            kt2 = self.sb(es, "a1_kt2", [64, 512], F32)
            krs = [self.sb(es, f"a1_krs{i}", [64, 512], BF16) for i in range(2)]
            ptr = [self.bank(es, f"a1_ptr{i}", BF16) for i in range(2)]
            pz = [self.bank(es, f"a1_pz{i}") for i in range(4)]
            pss = self.bank(es, "a1_pss")

            b_win = Buf("win")
            b_xt = [Buf() for _ in range(4)]
            b_sq, b_ss, b_rs = Buf(), Buf(), Buf()
            b_xn = [Buf(), Buf()]
            b_hT = [[Buf() for _ in range(16)] for _ in range(2)]
            b_posi, b_posf, b_ang, b_rt, b_rti = Buf(), Buf(), Buf(), Buf(), Buf()
            b_rope = [Buf(), Buf(), Buf()]
            b_raw, b_sqz, b_rbc = Buf(), Buf(), Buf()
            b_stg = [Buf(), Buf()]
            b_kt1, b_kt2 = Buf(), Buf()
            b_krs = [Buf(), Buf()]
            b_ptr = [Buf(psum=True), Buf(psum=True)]
            b_pz = [Buf(psum=True) for _ in range(4)]
            b_pss = Buf(psum=True)
            b_dram = Buf("a1_dram")

            wsrc = self.w_in.rearrange("(kc p) n -> p kc n", p=128)
            self.dma(POOL, win[:, :, 0:832], wsrc[:, :, 0:832], writes=[b_win])
            self.dma(POOL, win[:, :, 832:864], wsrc[:, :, 800:832], writes=[b_win])
            self.dma(POOL, win[:, :, 864:896], wsrc[:, :, 768:800], writes=[b_win])

            pz_i = 0
            import os as _os
            STOP = float(_os.environ.get("A1_STOP", "99"))
            def front(tb):
                nonlocal pz_i
                own = tb >= NBP
                hb = tb % 2
                t0 = tb * 512
                rb = tb % 3
                self.dma(SP, rope[rb][:], self.ROPE[:, :, t0:t0 + 512].rearrange("k p n -> p k n"), writes=[b_rope[rb]])

                if STOP <= 1:
                    return
                so = (tb % 2) * 4
                self.op(DVE, lambda: nc.vector.memset(ss[:, so:so + 4], 0.0), writes=[b_ss])
                for t in range(4):
                    self.dma(SP, xt[t][:], self.xall[t0 + t * 128:t0 + (t + 1) * 128, :], writes=[b_xt[t]])
                    self.op(ACT, lambda: nc.scalar.activation(out=sq[:], in_=xt[t][:], func=AF.Square, accum_out=ss[:, so + t:so + t + 1]),
                            reads=[b_xt[t], b_ss], writes=[b_sq, b_ss])
                self.op(DVE, lambda: nc.vector.tensor_scalar(out=rs[:, so:so + 4], in0=ss[:, so:so + 4], scalar1=1.0 / D, scalar2=EPS, op0=ALU.mult, op1=ALU.add),
                        reads=[b_ss], writes=[b_rs])
                self.rsqrt(rs[:, so:so + 4], b_rs)
                for t in range(4):
                    self.op(DVE, lambda: nc.vector.tensor_scalar(out=xn[hb][:, t, :], in0=xt[t][:], scalar1=rs[:, so + t:so + t + 1], scalar2=None, op0=ALU.mult),
                            reads=[b_xt[t], b_rs], writes=[b_xn[hb]])

            def backT(tb):
                nonlocal pz_i
                own = tb >= NBP
                hb = tb % 2
                t0 = tb * 512
                rb = tb % 3
                for fc in range(16):
                    pi = fc % 2
                    self.pre(PE, reads=[b_xn[hb], self.b_const], writes=[b_ptr[pi]])
                    for t in range(4):
                        ins = nc.tensor.transpose(out=ptr[pi][:, t * 128:(t + 1) * 128], in_=xn[hb][:, t, fc * 128:(fc + 1) * 128], identity=self.ident_bf[:])
                    self.post(PE, ins, reads=[b_xn[hb], self.b_const], writes=[b_ptr[pi]])
                    if STOP <= 3.2:
                        continue
                    if fc % 2 == 0:
                        self.op(DVE, lambda: nc.vector.tensor_scalar(out=hT[hb][:, fc, :], in0=ptr[pi][:, 0:512], scalar1=self.Am[:, fc:fc + 1], scalar2=self.adaT[:, fc:fc + 1], op0=ALU.mult, op1=ALU.add),
                                reads=[b_ptr[pi], self.b_AB, self.b_ada], writes=[b_hT[hb][fc]])
                    else:
                        self.op(ACT, lambda: nc.scalar.activation(out=hT[hb][:, fc, :], in_=ptr[pi][:, 0:512], func=AF.Identity, scale=self.Am[:, fc:fc + 1], bias=self.adaT[:, fc:fc + 1]),
                                reads=[b_ptr[pi], self.b_AB, self.b_ada], writes=[b_hT[hb][fc]])
                if STOP <= 3.5:
                    return
                _n = int(_os.environ.get("HTN", "16"))
                self.dma(POOL, self.HT[tb, :, 0:_n, :], hT[hb][:, 0:_n, :], reads=b_hT[hb], writes=[b_dram], is_output=("HT" in self.debug))

            def backP(tb):
                nonlocal pz_i
                own = tb >= NBP
                hb = tb % 2
                t0 = tb * 512
                rb = tb % 3
                if STOP <= 3:
                    return
                def proj_norm(col0, nch, gT, dst_ap_fn, inv_n):
                    nonlocal pz_i
                    for mc in range(nch):
                        p = pz_i % 4
                        pz_i += 1
                        self.pre(PE, reads=b_hT[hb] + [b_win], writes=[b_pz[p]])
                        for kc in range(16):
                            ins = nc.tensor.matmul(pz[p][:], lhsT=win[:, kc, col0 + mc * 128:col0 + (mc + 1) * 128], rhs=hT[hb][:, kc, :], start=(kc == 0), stop=(kc == 15))
                        self.post(PE, ins, reads=b_hT[hb] + [b_win], writes=[b_pz[p]])
                        if STOP <= 3.55:
                            continue
                        self.op(ACT, lambda: nc.scalar.activation(out=sqz[:, mc, :], in_=pz[p][:], func=AF.Square), reads=[b_pz[p]], writes=[b_sqz])
                        if STOP <= 3.57:
                            continue
                        self.op(DVE, lambda: nc.vector.tensor_scalar(out=raw[:, mc, :], in0=pz[p][:], scalar1=gT[:, mc:mc + 1], scalar2=None, op0=ALU.mult),
                                reads=[b_pz[p], self.b_const], writes=[b_raw])
                    if STOP <= 3.6:
                        return
                    self.pre(PE, reads=[b_sqz, self.b_const], writes=[b_pss])
                    for mc in range(nch):
                        ins = nc.tensor.matmul(pss[:], lhsT=self.ones_bf[:], rhs=sqz[:, mc, :], start=(mc == 0), stop=(mc == nch - 1))
                    self.post(PE, ins, reads=[b_sqz, self.b_const], writes=[b_pss])
                    self.op(DVE, lambda: nc.vector.tensor_scalar(out=rbc[:], in0=pss[:], scalar1=inv_n, scalar2=EPS, op0=ALU.mult, op1=ALU.add),
                            reads=[b_pss], writes=[b_rbc])
                    if STOP <= 3.7:
                        return
                    self.rsqrt(rbc[:], b_rbc)
                    if STOP <= 3.8:
                        return
                    si = tb % 2
                    for mc in range(nch):
                        self.op(DVE, lambda: nc.vector.tensor_tensor(out=stg[si][:, mc, :], in0=raw[:, mc, :], in1=rbc[:], op=ALU.mult),
                                reads=[b_raw, b_rbc], writes=[b_stg[si]])
                    if STOP <= 3.9:
                        return
                    dst_ap_fn(stg[si], b_stg[si])

                def st_ckv(st, bst):
                    self.dma(POOL, self.CKVN[:, :, t0:t0 + 512].rearrange("c p n -> p c n"), st[:, 0:2, :], reads=[bst], writes=[b_dram],
                             is_output=("CKVN" in self.debug))

                proj_norm(512, 2, self.gkvT, st_ckv, 1.0 / 256)
                if own:
                    o0 = t0 - NPRE

                    def st_cq(st, bst):
                        self.dma(POOL, self.CQN[:, :, o0:o0 + 512].rearrange("c p n -> p c n"), st[:, 0:4, :], reads=[bst], writes=[b_dram],
                                 is_output=("CQN" in self.debug))
                    proj_norm(0, 4, self.gqT, st_cq, 1.0 / 512)
                if STOP <= 4:
                    return
                pa = pz_i % 4
                pb = (pz_i + 1) % 4
                pz_i += 2
                for (p, c0) in ((pa, 768), (pb, 832)):
                    self.pre(PE, reads=b_hT[hb] + [b_win], writes=[b_pz[p]])
                    for kc in range(16):
                        ins = nc.tensor.matmul(pz[p][0:64, :], lhsT=win[:, kc, c0:c0 + 64], rhs=hT[hb][:, kc, :], start=(kc == 0), stop=(kc == 15))
                    self.post(PE, ins, reads=b_hT[hb] + [b_win], writes=[b_pz[p]])
                self.op(DVE, lambda: nc.vector.tensor_tensor(out=kt1[:], in0=pz[pa][0:64, :], in1=rope[rb][:, 0, :], op=ALU.mult),
                        reads=[b_pz[pa], b_rope[rb]], writes=[b_kt1])
                self.op(DVE, lambda: nc.vector.tensor_tensor(out=kt2[:], in0=pz[pb][0:64, :], in1=rope[rb][:, 1, :], op=ALU.mult),
                        reads=[b_pz[pb], b_rope[rb]], writes=[b_kt2])
                ki = tb % 2
                self.op(DVE, lambda: nc.vector.tensor_tensor(out=krs[ki][:], in0=kt1[:], in1=kt2[:], op=ALU.add),
                        reads=[b_kt1, b_kt2], writes=[b_krs[ki]])
                self.dma(POOL, self.KROPE[:, t0:t0 + 512], krs[ki][:], reads=[b_krs[ki]], writes=[b_dram], is_output=("KROPE" in self.debug))

            blks = list(getattr(self, "a1_blocks", range(NB)))
            nbk = len(blks)
            front(blks[0])
            if nbk > 1:
                front(blks[1])
            backT(blks[0])
            for bi, tb in enumerate(blks):
                if bi + 2 < nbk:
                    front(blks[bi + 2])
                if bi + 1 < nbk:
                    backT(blks[bi + 1])
                backP(tb)
            self.barrier()


    def phase_g(self):
        nc = self.nc
        PE, ACT, DVE, POOL, SP = self.PE, self.ACT, self.DVE, self.POOL, self.SP
        with ExitStack() as es:
            wq = self.sb(es, "g_wq", [128, 16, 512], BF16)
            wk = self.sb(es, "g_wk", [128, 16, 512], BF16)
            wv = self.sb(es, "g_wv", [128, 16, 1024], BF16)
            wa = self.sb(es, "g_wa", [128, 16, 16], BF16)
            wr = self.sb(es, "g_wr", [128, 16, 1024], BF16)
            wg = self.sb(es, "g_wg", [16, 512], BF16)
            keep = self.sb(es, "g_keep", [128, 2, 512], F32)
            mask128 = self.sb(es, "g_mask", [128, 128], F32)
            hT = self.sb(es, "g_hT", [128, 16, 512], BF16)
            vtok = self.sb(es, "g_vtok", [128, 4, 1024], BF16)
            aT = self.sb(es, "g_aT", [16, 512], BF16)
            lsb = self.sb(es, "g_l", [128, 512], F32)
            cs = self.sb(es, "g_cs", [128, 512], F32)
            E1 = self.sb(es, "g_E1", [128, 512], F32)
            E2 = self.sb(es, "g_E2", [128, 512], F32)
            dec = self.sb(es, "g_dec", [128, 8], F32)
            nb1 = self.sb(es, "g_nb1", [128, 2], F32)
            kinv_f = self.sb(es, "g_kinvf", [128, 512], F32)
            kinvT = self.sb(es, "g_kinvT", [128, 512], BF16)
            qdecT = self.sb(es, "g_qdecT", [128, 512], BF16)
            kdecT = self.sb(es, "g_kdecT", [128, 512], BF16)
            kdec_tok = self.sb(es, "g_kdtok", [128, 4, 128], BF16)
            attn_sb = self.sb(es, "g_attn", [128, 128], BF16)
            state = self.sb(es, "g_state", [128, 4, 256], F32)
            state_bf = self.sb(es, "g_statebf", [128, 4, 256], BF16)
            sq = self.sb(es, "g_sq", [128, 2, 512], BF16)
            rstd = self.sb(es, "g_rstd", [128, 512], F32)
            sig = self.sb(es, "g_sig", [128, 512], F32)
            srT = self.sb(es, "g_sr", [128, 2, 512], F32)
            tmp = self.sb(es, "g_tmp", [128, 512], F32)
            stg = [self.sb(es, f"g_stg{i}", [128, 2, 512], BF16) for i in range(2)]
            pl = [self.sb(es, f"g_pl{h}", [128, 512], F32) for h in range(4)]
            pcs = [self.sb(es, f"g_pcs{h}", [128, 512], F32) for h in range(4)]
            pkd = [self.sb(es, f"g_pkd{h}", [128, 512], BF16) for h in range(4)]
            pkt = [self.sb(es, f"g_pkt{h}", [128, 4, 128], BF16) for h in range(4)]
            pnb = self.sb(es, "g_pnb", [128, 8], F32)
            E1t = [E1, self.sb(es, "g_E1b", [128, 512], F32)]
            E2t = [E2, self.sb(es, "g_E2b", [128, 512], F32)]
            kinvf = [kinv_f, self.sb(es, "g_kinvfb", [128, 512], F32)]
            kdt = [kdecT, self.sb(es, "g_kdtb", [128, 512], BF16)]
            qd = [qdecT] + [self.sb(es, f"g_qd{i}", [128, 512], BF16) for i in range(1, 4)]
            attn4 = [attn_sb] + [self.sb(es, f"g_attn{i}", [128, 128], BF16) for i in range(1, 4)]
            dec4 = self.sb(es, "g_dec4", [128, 4, 8], F32)
            rstd2 = [rstd, self.sb(es, "g_rstdb", [128, 512], F32)]
            sig2 = [sig, self.sb(es, "g_sigb", [128, 512], F32)]
            tmp2 = [tmp, self.sb(es, "g_tmpb", [128, 512], F32)]
            b_E1t, b_E2t, b_kinvfL, b_kdt = ([Buf(), Buf()] for _ in range(4))
            b_qd = [Buf() for _ in range(4)]
            b_attn4 = [Buf() for _ in range(4)]
            b_dec4 = [Buf() for _ in range(4)]
            b_rstd2, b_sig2, b_tmp2 = ([Buf(), Buf()] for _ in range(3))
            b_pl = [Buf() for _ in range(4)]
            b_pcs = [Buf() for _ in range(4)]
            b_pkd = [Buf() for _ in range(4)]
            b_pkt = [Buf() for _ in range(4)]
            b_pnb = [Buf() for _ in range(4)]
            banks = [self.bank(es, f"g_bank{i}", BF16 if i == 3 else F32) for i in range(8)]
            bA, bB, bG, bTR, bAT, bDS, bO0, bO1 = banks
            pb = [Buf(psum=True) for _ in range(8)]
            pA, pB, pG, pTR, pAT, pDS, pO0, pO1 = pb
            b_w, b_hT, b_vtok, b_aT, b_l, b_cs, b_E1, b_E2, b_dec, b_nb1 = (Buf() for _ in range(10))
            b_kinvf, b_kinvT, b_qdecT, b_kdecT, b_kdtok, b_attn = (Buf() for _ in range(6))
            b_state = [Buf() for _ in range(4)]
            b_statebf = [Buf() for _ in range(4)]
            b_sq, b_rstd, b_sig, b_sr, b_tmp = (Buf() for _ in range(5))
            b_stg = [Buf(), Buf()]
            b_dram = Buf()
            bc = self.b_const

            wsrc = self.w_in.rearrange("(kc p) n -> p kc n", p=128)
            self.dma(POOL, wk[:], wsrc[:, :, C_KG:C_KG + 512], writes=[b_w])
            self.dma(POOL, wv[:], wsrc[:, :, C_VG:C_VG + 1024], writes=[b_w])
            self.dma(POOL, wa[:], wsrc[:, :, C_AG:C_AG + 16], writes=[b_w])
            self.dma(POOL, wg[:], self.w_gate_up[:, :], writes=[b_w])
            self.dma(POOL, wq[:], wsrc[:, :, C_QG:C_QG + 512], writes=[b_w])
            self.dma(POOL, wr[:], wsrc[:, :, C_RG:C_RG + 1024], writes=[b_w])
            self.dma(SP, keep[:, 0, :], self.c_keep[:, :], writes=[b_w])
            self.dma(SP, mask128[:], self.c_mask128[:, :], writes=[b_w])
            if "FFN" in self.phases:
                self.precast_ffn()
            self.op(DVE, lambda: nc.vector.memset(keep[:, 1, :], 1.0), writes=[b_w])
            for h in range(4):
                self.op(DVE, lambda: nc.vector.memset(state[:, h, :], 0.0), writes=[b_state[h]])
                self.op(DVE, lambda: nc.vector.memset(state_bf[:, h, :], 0.0), writes=[b_statebf[h]])

            def proj(bank, pbuf, wt, c0, m):
                self.pre(PE, reads=[b_hT, b_w], writes=[pbuf])
                for kc in range(16):
                    ins = nc.tensor.matmul(bank[0:m, :], lhsT=wt[:, kc, c0:c0 + m], rhs=hT[:, kc, :], start=(kc == 0), stop=(kc == 15))
                self.post(PE, ins, reads=[b_hT, b_w], writes=[pbuf])

            stg_i = 0
            for tb in getattr(self, "g_blocks", range(NB)):
                own = tb >= NBP
                slot = tb // 4
                t0 = tb * 512
                self.dma(SP, hT[:], self.HT[tb], writes=[b_hT])
                for t in range(4):
                    for half in range(2):
                        bank, pbuf = (bA, pA) if half == 0 else (bB, pB)
                        self.pre(PE, reads=[b_hT, b_w], writes=[pbuf])
                        for kc in range(16):
                            ins = nc.tensor.matmul(bank[:], lhsT=hT[:, kc, t * 128:(t + 1) * 128], rhs=wv[:, kc, half * 512:(half + 1) * 512], start=(kc == 0), stop=(kc == 15))
                        self.post(PE, ins, reads=[b_hT, b_w], writes=[pbuf])
                        self.op(ACT, lambda: nc.scalar.copy(out=vtok[:, t, half * 512:(half + 1) * 512], in_=bank[:]), reads=[pbuf], writes=[b_vtok])
                proj(bB, pB, wa, 0, 16)
                self.op(ACT, lambda: nc.scalar.copy(out=aT[:], in_=bB[0:16, :]), reads=[pB], writes=[b_aT])
                if not own:
                    Kb = [banks[2], banks[4], banks[5], banks[6]]
                    Kp = [pb[2], pb[4], pb[5], pb[6]]
                    for h in range(4):
                        proj(Kb[h], Kp[h], wk, h * 128, 128)
                    for h in range(4):
                        gbk, gpb = (bA, pA) if h % 2 == 0 else (bB, pB)
                        self.op(PE, lambda: nc.tensor.matmul(gbk[:], lhsT=wg[:, h * 128:(h + 1) * 128], rhs=aT[:], start=True, stop=True),
                                reads=[b_w, b_aT], writes=[gpb])
                        self.op(ACT, lambda: nc.scalar.activation(out=pl[h][:], in_=gbk[:], func=AF.Exp, scale=-1.0, bias=self.negb[:, h:h + 1]),
                                reads=[gpb, bc], writes=[b_pl[h]])
                    for h in range(4):
                        self.op(ACT, lambda: nc.scalar.activation(out=pl[h][:], in_=pl[h][:], func=AF.Ln, bias=1.0), reads=[b_pl[h]], writes=[b_pl[h]])
                    for h in range(4):
                        self.op(DVE, lambda: nc.vector.tensor_tensor_scan(out=pcs[h][:], data0=keep[:, 1, :], data1=pl[h][:], initial=0.0, op0=ALU.mult, op1=ALU.add),
                                reads=[b_pl[h], b_w], writes=[b_pcs[h]])
                        self.op(DVE, lambda: nc.vector.tensor_scalar(out=pnb[:, 2 * h:2 * h + 1], in0=pcs[h][:, 511:512], scalar1=-1.0 / 16, scalar2=None, op0=ALU.mult),
                                reads=[b_pcs[h]], writes=[b_pnb[h]])
                    for h in range(4):
                        self.op(ACT, lambda: nc.scalar.activation(out=pcs[h][:], in_=pcs[h][:], func=AF.Exp, scale=1.0 / 16, bias=pnb[:, 2 * h:2 * h + 1]),
                                reads=[b_pcs[h], b_pnb[h]], writes=[b_pcs[h]])
                        self.op(ACT, lambda: nc.scalar.activation(out=pnb[:, 2 * h + 1:2 * h + 2], in_=pnb[:, 2 * h:2 * h + 1], func=AF.Exp), reads=[b_pnb[h]], writes=[b_pnb[h]])
                    for h in range(4):
                        self.op(DVE, lambda: nc.vector.scalar_tensor_tensor(out=pkd[h][:], in0=Kb[h][:], scalar=self.smask[:, 4 + slot:5 + slot], in1=pcs[h][:], op0=ALU.mult, op1=ALU.mult),
                                reads=[Kp[h], b_pcs[h], bc], writes=[b_pkd[h]])
                    for h in range(4):
                        self.pre(PE, reads=[b_pkd[h], bc], writes=[pTR])
                        for t in range(4):
                            ins = nc.tensor.transpose(out=bTR[:, t * 128:(t + 1) * 128], in_=pkd[h][:, t * 128:(t + 1) * 128], identity=self.ident_bf[:])
                        self.post(PE, ins, reads=[b_pkd[h], bc], writes=[pTR])
                        if h % 2 == 0:
                            self.op(ACT, lambda: nc.scalar.copy(out=pkt[h][:].rearrange("p t d -> p (t d)"), in_=bTR[:, 0:512]), reads=[pTR], writes=[b_pkt[h]])
                        else:
                            self.op(DVE, lambda: nc.vector.tensor_copy(out=pkt[h][:].rearrange("p t d -> p (t d)"), in_=bTR[:, 0:512]), reads=[pTR], writes=[b_pkt[h]])
                    for h in range(4):
                        self.pre(PE, reads=[b_pkt[h], b_vtok], writes=[pDS])
                        for t in range(4):
                            ins = nc.tensor.matmul(bDS[:, 0:256], lhsT=pkt[h][:, t, :], rhs=vtok[:, t, h * 256:(h + 1) * 256], start=(t == 0), stop=(t == 3))
                        self.post(PE, ins, reads=[b_pkt[h], b_vtok], writes=[pDS])
                        self.op(DVE, lambda: nc.vector.scalar_tensor_tensor(out=state[:, h, :], in0=state[:, h, :], scalar=pnb[:, 2 * h + 1:2 * h + 2], in1=bDS[:, 0:256], op0=ALU.mult, op1=ALU.add),
                                reads=[b_state[h], b_pnb[h], pDS], writes=[b_state[h]])
                        if tb == NBP - 1 or (tb + 1) not in list(getattr(self, "g_blocks", range(NB))):
                            self.op(ACT, lambda: nc.scalar.copy(out=state_bf[:, h, :], in_=state[:, h, :]), reads=[b_state[h]], writes=[b_statebf[h]])
                    continue
                Qb = [banks[4], banks[5]]
                Qp = [pb[4], pb[5]]
                Kb2 = [bA, bB]
                Kp2 = [pA, pB]
                for h in range(4):
                    a2 = h % 2
                    proj(Kb2[a2], Kp2[a2], wk, h * 128, 128)
                    proj(Qb[a2], Qp[a2], wq, h * 128, 128)
                    self.op(PE, lambda: nc.tensor.matmul(bG[:], lhsT=wg[:, h * 128:(h + 1) * 128], rhs=aT[:], start=True, stop=True),
                            reads=[b_w, b_aT], writes=[pG])
                    self.op(ACT, lambda: nc.scalar.activation(out=pl[h][:], in_=bG[:], func=AF.Exp, scale=-1.0, bias=self.negb[:, h:h + 1]),
                            reads=[pG, bc], writes=[b_pl[h]])
                    self.op(ACT, lambda: nc.scalar.activation(out=pl[h][:], in_=pl[h][:], func=AF.Ln, bias=1.0), reads=[b_pl[h]], writes=[b_pl[h]])
                    self.op(DVE, lambda: nc.vector.tensor_tensor_scan(out=pcs[h][:], data0=keep[:, 0, :], data1=pl[h][:], initial=0.0, op0=ALU.mult, op1=ALU.add),
                            reads=[b_pl[h], b_w], writes=[b_pcs[h]])
                    self.op(ACT, lambda: nc.scalar.activation(out=E1t[a2][:], in_=pcs[h][:], func=AF.Exp, scale=1.0 / 16), reads=[b_pcs[h]], writes=[b_E1t[a2]])
                    self.op(ACT, lambda: nc.scalar.activation(out=E2t[a2][:], in_=pcs[h][:], func=AF.Exp, scale=-1.0 / 16), reads=[b_pcs[h]], writes=[b_E2t[a2]])
                    cs3 = pcs[h][:].rearrange("p (c j) -> p c j", j=64)
                    self.op(ACT, lambda: nc.scalar.activation(out=dec4[:, h, :].rearrange("p (c o) -> p c o", o=1), in_=cs3[:, :, 63:64], func=AF.Exp, scale=-1.0 / 16),
                            reads=[b_pcs[h]], writes=[b_dec4[h]])
                    self.op(DVE, lambda: nc.vector.tensor_tensor(out=kinvf[a2][:], in0=Kb2[a2][:], in1=E1t[a2][:], op=ALU.mult), reads=[Kp2[a2], b_E1t[a2]], writes=[b_kinvfL[a2]])
                    self.op(ACT, lambda: nc.scalar.copy(out=pkd[h][:], in_=kinvf[a2][:]), reads=[b_kinvfL[a2]], writes=[b_pkd[h]])
                    self.op(DVE, lambda: nc.vector.tensor_tensor(out=kdt[a2][:].rearrange("p (c j) -> p c j", j=64), in0=kinvf[a2][:].rearrange("p (c j) -> p c j", j=64),
                                                                in1=dec4[:, h, :].rearrange("p (c o) -> p c o", o=1).to_broadcast([128, 8, 64]), op=ALU.mult),
                            reads=[b_kinvfL[a2], b_dec4[h]], writes=[b_kdt[a2]])
                    self.op(DVE, lambda: nc.vector.scalar_tensor_tensor(out=qd[h][:], in0=Qb[a2][:], scalar=float(128 ** -0.5), in1=E2t[a2][:], op0=ALU.mult, op1=ALU.mult),
                            reads=[Qp[a2], b_E2t[a2]], writes=[b_qd[h]])
                    self.pre(PE, reads=[b_kdt[a2], bc], writes=[pTR])
                    for t in range(4):
                        ins = nc.tensor.transpose(out=bTR[:, t * 128:(t + 1) * 128], in_=kdt[a2][:, t * 128:(t + 1) * 128], identity=self.ident_bf[:])
                    self.post(PE, ins, reads=[b_kdt[a2], bc], writes=[pTR])
                    if h % 2 == 0:
                        self.op(ACT, lambda: nc.scalar.copy(out=pkt[h][:].rearrange("p t d -> p (t d)"), in_=bTR[:, 0:512]), reads=[pTR], writes=[b_pkt[h]])
                    else:
                        self.op(DVE, lambda: nc.vector.tensor_copy(out=pkt[h][:].rearrange("p t d -> p (t d)"), in_=bTR[:, 0:512]), reads=[pTR], writes=[b_pkt[h]])
                Ob4 = [banks[4], banks[5], banks[6], banks[7]]
                Op4 = [pb[4], pb[5], pb[6], pb[7]]
                ATb = [bA, bB]
                ATp = [pA, pB]
                for t in range(4):
                    tc = slice(t * 128, (t + 1) * 128)
                    for h in range(4):
                        a2 = h % 2
                        self.op(PE, lambda: nc.tensor.matmul(ATb[a2][:, 0:128], lhsT=pkd[h][:, tc], rhs=qd[h][:, tc], start=True, stop=True),
                                reads=[b_pkd[h], b_qd[h]], writes=[ATp[a2]])
                        self.op(DVE, lambda: nc.vector.tensor_tensor(out=attn4[h][:], in0=ATb[a2][:, 0:128], in1=mask128[:], op=ALU.mult), reads=[ATp[a2], b_w], writes=[b_attn4[h]])
                    for half in range(2):
                        c = t * 2 + half
                        cc = slice(t * 128 + half * 64, t * 128 + half * 64 + 64)
                        hs = slice(half * 64, half * 64 + 64)
                        for h in range(4):
                            self.pre(PE, reads=[b_vtok, b_attn4[h], b_statebf[h], b_qd[h]], writes=[Op4[h]])
                            for vc in range(2):
                                vs = slice(h * 256 + vc * 128, h * 256 + (vc + 1) * 128)
                                oc = slice(vc * 128 + half * 64, vc * 128 + half * 64 + 64)
                                nc.tensor.matmul(Ob4[h][:, oc], lhsT=vtok[:, t, vs], rhs=attn4[h][:, hs], start=True, stop=False)
                                ins = nc.tensor.matmul(Ob4[h][:, oc], lhsT=state_bf[:, h, vc * 128:(vc + 1) * 128], rhs=qd[h][:, cc], start=False, stop=True)
                            self.post(PE, ins, reads=[b_vtok, b_attn4[h], b_statebf[h], b_qd[h]], writes=[Op4[h]])
                            self.op(PE, lambda: nc.tensor.matmul(bG[:, 0:256], lhsT=pkt[h][hs, t, :], rhs=vtok[hs, t, h * 256:(h + 1) * 256], start=True, stop=True),
                                    reads=[b_pkt[h], b_vtok], writes=[pG])
                            self.op(DVE, lambda: nc.vector.scalar_tensor_tensor(out=state[:, h, :], in0=state[:, h, :], scalar=dec4[:, h, c:c + 1], in1=bG[:, 0:256], op0=ALU.mult, op1=ALU.add),
                                    reads=[b_state[h], b_dec4[h], pG], writes=[b_state[h]])
                            self.op(ACT, lambda: nc.scalar.copy(out=state_bf[:, h, :], in_=state[:, h, :]), reads=[b_state[h]], writes=[b_statebf[h]])
                    for h in range(4):
                        self.op(ACT, lambda: nc.scalar.copy(out=pl[h][:, tc], in_=Ob4[h][:, 0:128]), reads=[Op4[h]], writes=[b_pl[h]])
                        self.op(DVE, lambda: nc.vector.tensor_copy(out=pcs[h][:, tc], in_=Ob4[h][:, 128:256]), reads=[Op4[h]], writes=[b_pcs[h]])
                for h in range(4):
                    a2 = h % 2
                    osb = (pl[h], pcs[h])
                    osbuf = (b_pl[h], b_pcs[h])
                    for vc in range(2):
                        self.op(ACT, lambda: nc.scalar.activation(out=sq[:, vc, :], in_=osb[vc][:], func=AF.Square), reads=[osbuf[vc]], writes=[b_sq])
                    self.pre(PE, reads=[b_sq, bc], writes=[pG])
                    for vc in range(2):
                        ins = nc.tensor.matmul(bG[:], lhsT=self.ones_bf[:], rhs=sq[:, vc, :], start=(vc == 0), stop=(vc == 1))
                    self.post(PE, ins, reads=[b_sq, bc], writes=[pG])
                    self.op(DVE, lambda: nc.vector.tensor_scalar(out=rstd2[a2][:], in0=bG[:], scalar1=1.0 / 256, scalar2=EPS, op0=ALU.mult, op1=ALU.add), reads=[pG], writes=[b_rstd2[a2]])
                    self.rsqrt(rstd2[a2][:], b_rstd2[a2])
                    si = stg_i % 2
                    stg_i += 1
                    for vc in range(2):
                        rbank, rbuf = (bA, pA) if vc == 0 else (bB, pB)
                        proj(rbank, rbuf, wr, h * 256 + vc * 128, 128)
                        self.op(ACT, lambda: nc.scalar.activation(out=sig2[vc][:], in_=rbank[:], func=AF.Exp, scale=-1.0), reads=[rbuf], writes=[b_sig2[vc]])
                        self.op(DVE, lambda: nc.vector.tensor_scalar(out=sig2[vc][:], in0=sig2[vc][:], scalar1=1.0, scalar2=None, op0=ALU.add), reads=[b_sig2[vc]], writes=[b_sig2[vc]])
                        self.op(DVE, lambda: nc.vector.reciprocal(out=sig2[vc][:], in_=sig2[vc][:]), reads=[b_sig2[vc]], writes=[b_sig2[vc]])
                        self.op(DVE, lambda: nc.vector.tensor_tensor(out=srT[:, vc, :], in0=rbank[:], in1=sig2[vc][:], op=ALU.mult), reads=[rbuf, b_sig2[vc]], writes=[b_sr])
                        self.op(DVE, lambda: nc.vector.scalar_tensor_tensor(out=tmp2[vc][:], in0=osb[vc][:], scalar=self.gglaT[:, vc:vc + 1], in1=rstd2[a2][:], op0=ALU.mult, op1=ALU.mult),
                                reads=[osbuf[vc], b_rstd2[a2], bc], writes=[b_tmp2[vc]])
                        self.op(DVE, lambda: nc.vector.tensor_tensor(out=stg[si][:, vc, :], in0=tmp2[vc][:], in1=srT[:, vc, :], op=ALU.mult), reads=[b_tmp2[vc], b_sr], writes=[b_stg[si]])
                    o0 = t0 - NPRE
                    self.dma(POOL, self.MIXT[8 + 2 * h:10 + 2 * h, :, o0:o0 + 512].rearrange("c p n -> p c n"), stg[si][:], reads=[b_stg[si]], writes=[b_dram],
                             is_output=("MIXT" in self.debug))
            self.barrier()


    def phase_att(self):
        nc = self.nc
        PE, ACT, DVE, POOL, SP = self.PE, self.ACT, self.DVE, self.POOL, self.SP
        SC = float(192 ** -0.5)
        with ExitStack() as es:
            ckv = self.sb(es, "t_ckv", [128, 2, NT], BF16)
            krp = self.sb(es, "t_krp", [128, NT], BF16)
            cqn = self.sb(es, "t_cqn", [128, 4, NOWN], BF16)
            wuq = self.sb(es, "t_wuq", [128, 4, 1536], BF16)
            wsw = self.sb(es, "t_wsw", [128, 4, 8, 64], BF16)
            wuk = self.sb(es, "t_wuk", [128, 2, 1024], BF16)
            wuv = self.sb(es, "t_wuv", [128, 2, 1024], BF16)
            rope = self.sb(es, "t_rope", [64, 2, NOWN], F32)
            tri = self.sb(es, "t_tri", [128, 128], BF16)
            KT = self.sb(es, "t_KT", [128, NT], BF16)
            V = self.sb(es, "t_V", [128, 64, 128], BF16)
            qn = self.sb(es, "t_qn", [128, NOWN], BF16)
            qr = self.sb(es, "t_qr", [128, NOWN], BF16)
            t1 = self.sb(es, "t_t1", [64, 512], F32)
            t2 = self.sb(es, "t_t2", [64, 512], F32)
            rl = self.sb(es, "t_rl", [128, 512], F32)
            ost = [self.sb(es, f"t_ost{i}", [128, 512], BF16) for i in range(2)]
            SP_b = [self.ps(es, f"t_spair{i}", [128, 1024], F32) for i in range(3)]
            SP_p = [Buf(psum=True) for _ in range(3)]
            banks = [self.bank(es, f"t_bank{i}") for i in range(2)]
            pbs = [Buf(psum=True) for _ in range(2)]
            O_b, O_p = [banks[0], banks[0]], [pbs[0], pbs[0]]
            L_b, L_p = [banks[1], banks[1]], [pbs[1], pbs[1]]
            PT2 = [self.sb(es, f"t_PT2_{i}", [128, 1024], BF16) for i in range(3)]
            b_PT2 = [Buf() for _ in range(3)]
            PS2 = [self.sb(es, f"t_PS2_{i}", [128, 512], BF16) for i in range(3)]
            b_PS2 = [Buf() for _ in range(3)]
            osb = self.sb(es, "t_osb", [128, 512], F32)
            lsb2 = self.sb(es, "t_lsb2", [128, 512], F32)
            b_osb, b_lsb2 = Buf(), Buf()
            b_in, b_w, b_KT, b_V, b_qn, b_qr, b_t1, b_t2, b_rl = (Buf() for _ in range(9))
            b_ost = [Buf(), Buf()]
            b_dram = Buf()
            bc = self.b_const

            self.dma(SP, ckv[:], self.CKVN.rearrange("c p n -> p c n"), writes=[b_in])
            self.op(DVE, lambda: nc.vector.memset(krp[64:128, :], 0.0), writes=[b_in])
            self.op(DVE, lambda: nc.vector.memset(qr[64:128, :], 0.0), writes=[b_qr])
            self.dma(SP, krp[0:64, :], self.KROPE[:, :], writes=[b_in])
            self.dma(SP, cqn[:], self.CQN.rearrange("c p n -> p c n"), writes=[b_in])
            self.dma(SP, rope[:], self.ROPE[:, :, NPRE:NT].rearrange("k p n -> p k n"), writes=[b_in])
            self.dma(SP, tri[:], self.c_tri_bf[:, :], writes=[b_in])
            self.dma(POOL, wuq[:], self.w_uq.rearrange("(kc p) n -> p kc n", p=128), writes=[b_w])
            wq4 = self.w_uq.rearrange("(kc p) (h c) -> p kc h c", p=128, c=192)
            for kc in range(4):
                self.dma(POOL, wsw[:, kc, :, 0:32], wq4[:, kc, :, 160:192], writes=[b_w])
                self.dma(POOL, wsw[:, kc, :, 32:64], wq4[:, kc, :, 128:160], writes=[b_w])
            self.dma(POOL, wuk[:], self.w_uk.rearrange("(kc p) n -> p kc n", p=128), writes=[b_w])
            self.dma(POOL, wuv[:], self.w_uv.rearrange("(kc p) n -> p kc n", p=128), writes=[b_w])
            self.op(DVE, lambda: nc.vector.tensor_scalar(out=rope[:], in0=rope[:], scalar1=SC, scalar2=None, op0=ALU.mult), reads=[b_in], writes=[b_in])

            ev = [0]

            def evac(out_ap, in_ap, pbuf, wbuf, scale=None):
                ev[0] += 1
                if ev[0] % 2 == 0:
                    if scale is None:
                        self.op(DVE, lambda: nc.vector.tensor_copy(out=out_ap, in_=in_ap), reads=[pbuf], writes=[wbuf])
                    else:
                        self.op(DVE, lambda: nc.vector.tensor_scalar(out=out_ap, in0=in_ap, scalar1=scale, scalar2=None, op0=ALU.mult), reads=[pbuf], writes=[wbuf])
                else:
                    if scale is None:
                        self.op(ACT, lambda: nc.scalar.copy(out=out_ap, in_=in_ap), reads=[pbuf], writes=[wbuf])
                    else:
                        self.op(ACT, lambda: nc.scalar.mul(out=out_ap, in_=in_ap, mul=scale), reads=[pbuf], writes=[wbuf])

            gb = [0]

            def genbank():
                gb[0] += 1
                i = gb[0] % 3
                return (SP_b[i][:, 0:512], SP_p[i])

            qt_n = 0
            ost_i = 0
            for h in getattr(self, "att_heads", range(8)):
                for blk in range(NB):
                    bank, pbuf = genbank()
                    self.pre(PE, reads=[b_in, b_w], writes=[pbuf])
                    for c in range(2):
                        ins = nc.tensor.matmul(bank[:], lhsT=wuk[:, c, h * 128:(h + 1) * 128], rhs=ckv[:, c, blk * 512:(blk + 1) * 512], start=(c == 0), stop=(c == 1))
                    self.post(PE, ins, reads=[b_in, b_w], writes=[pbuf])
                    evac(KT[:, blk * 512:(blk + 1) * 512], bank[:], pbuf, b_KT)
                for g in range(16):
                    bank, pbuf = genbank()
                    self.pre(PE, reads=[b_in, b_w], writes=[pbuf])
                    for j in range(4):
                        tl = g * 4 + j
                        for c in range(2):
                            ins = nc.tensor.matmul(bank[:, j * 128:(j + 1) * 128], lhsT=ckv[:, c, tl * 128:(tl + 1) * 128], rhs=wuv[:, c, h * 128:(h + 1) * 128], start=(c == 0), stop=(c == 1))
                    self.post(PE, ins, reads=[b_in, b_w], writes=[pbuf])
                    evac(V[:, g * 4:(g + 1) * 4, :], bank[:].rearrange("p (j d) -> p j d", d=128), pbuf, b_V)
                for blk in range(4):
                    qc = slice(blk * 512, (blk + 1) * 512)
                    bank, pbuf = genbank()
                    self.pre(PE, reads=[b_in, b_w], writes=[pbuf])
                    for kc in range(4):
                        ins = nc.tensor.matmul(bank[:], lhsT=wuq[:, kc, h * 192:h * 192 + 128], rhs=cqn[:, kc, qc], start=(kc == 0), stop=(kc == 3))
                    self.post(PE, ins, reads=[b_in, b_w], writes=[pbuf])
                    evac(qn[:, qc], bank[:], pbuf, b_qn, scale=SC)
                    bank1, pbuf1 = genbank()
                    self.pre(PE, reads=[b_in, b_w], writes=[pbuf1])
                    for kc in range(4):
                        ins = nc.tensor.matmul(bank1[0:64, :], lhsT=wuq[:, kc, h * 192 + 128:h * 192 + 192], rhs=cqn[:, kc, qc], start=(kc == 0), stop=(kc == 3))
                    self.post(PE, ins, reads=[b_in, b_w], writes=[pbuf1])
                    self.op(DVE, lambda: nc.vector.tensor_tensor(out=t1[:], in0=bank1[0:64, :], in1=rope[:, 0, qc], op=ALU.mult), reads=[pbuf1, b_in], writes=[b_t1])
                    bank2, pbuf2 = genbank()
                    self.pre(PE, reads=[b_in, b_w], writes=[pbuf2])
                    for kc in range(4):
                        ins = nc.tensor.matmul(bank2[0:64, :], lhsT=wsw[:, kc, h, :], rhs=cqn[:, kc, qc], start=(kc == 0), stop=(kc == 3))
                    self.post(PE, ins, reads=[b_in, b_w], writes=[pbuf2])
                    self.op(DVE, lambda: nc.vector.tensor_tensor(out=t2[:], in0=bank2[0:64, :], in1=rope[:, 1, qc], op=ALU.mult), reads=[pbuf2, b_in], writes=[b_t2])
                    self.op(DVE, lambda: nc.vector.tensor_tensor(out=qr[0:64, qc], in0=t1[:], in1=t2[:], op=ALU.add), reads=[b_t1, b_t2], writes=[b_qr])

                for i in getattr(self, "att_qtiles", range(4)):
                    q0 = i * 512
                    oi = qt_n % 2
                    qt_n += 1
                    Ob, Op, Lb, Lp = O_b[oi], O_p[oi], L_b[oi], L_p[oi]
                    groups = [([0, 1], 0, 0, False)]
                    for c in range(0, 4 * i, 2):
                        groups.append(([48 + c, 48 + c + 1], 0, 3, False))
                    for j in range(4):
                        groups.append(([48 + 4 * i + j], 128 * j, 3, True))
                    for c in range(2, 48, 2):
                        groups.append(([c, c + 1], 0, c // 16, False))
                    n = len(groups)

                    def issue_S(g):
                        cl, f0, bcol, dg = groups[g]
                        sp = g % 3
                        self.pre(PE, reads=[b_KT, b_in, b_qn, b_qr], writes=[SP_p[sp]])
                        for idx, kc_ in enumerate(cl):
                            ks = slice(kc_ * 128, (kc_ + 1) * 128)
                            o_ = idx * 512
                            nc.tensor.matmul(SP_b[sp][:, o_ + f0:o_ + 512], lhsT=KT[:, ks], rhs=qn[:, q0 + f0:q0 + 512], start=True, stop=False)
                            ins = nc.tensor.matmul(SP_b[sp][:, o_ + f0:o_ + 512], lhsT=krp[:, ks], rhs=qr[:, q0 + f0:q0 + 512], start=False, stop=True)
                        self.post(PE, ins, reads=[b_KT, b_in, b_qn, b_qr], writes=[SP_p[sp]])

                    issue_S(0)
                    issue_S(1)
                    for g in range(n):
                        cl, f0, bcol, dg = groups[g]
                        sp = g % 3
                        w = 512 * len(cl)
                        if g + 2 < n:
                            issue_S(g + 2)
                        self.op(ACT, lambda: nc.scalar.activation(out=PT2[sp][:, f0:w], in_=SP_b[sp][:, f0:w], func=AF.Exp, bias=self.smask[:, bcol:bcol + 1]),
                                reads=[SP_p[sp], bc], writes=[b_PT2[sp]])
                        if dg:
                            self.op(DVE, lambda: nc.vector.tensor_tensor(out=PT2[sp][:, f0:f0 + 128], in0=PT2[sp][:, f0:f0 + 128], in1=tri[:], op=ALU.mult),
                                    reads=[b_PT2[sp], b_in], writes=[b_PT2[sp]])
                        if len(cl) == 2:
                            self.op(DVE, lambda: nc.vector.tensor_tensor(out=PS2[sp][:], in0=PT2[sp][:, 0:512], in1=PT2[sp][:, 512:1024], op=ALU.add),
                                    reads=[b_PT2[sp]], writes=[b_PS2[sp]])
                        self.pre(PE, reads=[b_V, b_PT2[sp]], writes=[Op])
                        for idx, kc_ in enumerate(cl):
                            o_ = idx * 512
                            first = (g == 0 and idx == 0)
                            last = (g == n - 1 and idx == len(cl) - 1)
                            ins = nc.tensor.matmul(Ob[:, f0:512], lhsT=V[:, kc_, :], rhs=PT2[sp][:, o_ + f0:o_ + 512], start=first, stop=last)
                        self.post(PE, ins, reads=[b_V, b_PT2[sp]], writes=[Op])
                        if len(cl) == 2:
                            self.op(PE, lambda: nc.tensor.matmul(Lb[:], lhsT=self.ones_bf[:], rhs=PS2[sp][:], start=(g == 0), stop=(g == n - 1)),
                                    reads=[b_PS2[sp], bc], writes=[Lp])
                        else:
                            self.op(PE, lambda: nc.tensor.matmul(Lb[:, f0:512], lhsT=self.ones_bf[:], rhs=PT2[sp][:, f0:512], start=False, stop=False),
                                    reads=[b_PT2[sp], bc], writes=[Lp])
                    self.op(ACT, lambda: nc.scalar.copy(out=lsb2[:], in_=Lb[:]), reads=[Lp], writes=[b_lsb2])
                    self.op(ACT, lambda: nc.scalar.copy(out=osb[:], in_=Ob[:]), reads=[Op], writes=[b_osb])
                    self.op(DVE, lambda: nc.vector.reciprocal(out=rl[:], in_=lsb2[:]), reads=[b_lsb2], writes=[b_rl])
                    oi2 = ost_i % 2
                    ost_i += 1
                    self.op(DVE, lambda: nc.vector.tensor_tensor(out=ost[oi2][:], in0=osb[:], in1=rl[:], op=ALU.mult), reads=[b_osb, b_rl], writes=[b_ost[oi2]])
                    self.dma(POOL, self.MIXT[h, :, q0:q0 + 512], ost[oi2][:], reads=[b_ost[oi2]], writes=[b_dram], is_output=("MIXT" in self.debug))
            self.barrier()


    def precast_ffn(self):
        POOL = self.POOL
        self.b_wffn = Buf("wffn")
        for src, dst in ((self.w_ffn_gate, self.WG), (self.w_ffn_up, self.WU), (self.w_ffn_down, self.WD)):
            s3 = src.rearrange("(a p) n -> p a n", p=128)
            d3 = dst.rearrange("(a p) n -> p a n", p=128)
            na = s3.shape[1]
            step = 4
            for a0 in range(0, na, step):
                a1 = min(na, a0 + step)
                self.dma(POOL, d3[:, a0:a1, :], s3[:, a0:a1, :], writes=[self.b_wffn])

    def phase_out(self):
        nc = self.nc
        PE, ACT, DVE, POOL, SP = self.PE, self.ACT, self.DVE, self.POOL, self.SP
        with ExitStack() as es:
            wo = self.sb(es, "o_wo", [128, 16, D], BF16)
            gm = self.sb(es, "o_gm", [128, D], F32)
            mixT = [self.sb(es, f"o_mixT{i}", [128, 16, 512], BF16) for i in range(2)]
            xt = [self.sb(es, f"o_xt{i}", [128, D], F32) for i in range(2)]
            x1 = [self.sb(es, f"o_x1{i}", [128, D], F32) for i in range(2)]
            tmp = self.sb(es, "o_tmp", [128, D], F32)
            sqj = self.sb(es, "o_sqj", [128, D], BF16)
            ss4 = self.sb(es, "o_ss4", [128, 8], F32)
            st = self.sb(es, "o_st", [128, 8], F32)
            xn2 = self.sb(es, "o_xn2", [128, 4, D], BF16)
            h2T = [self.sb(es, f"o_h2T{i}", [128, 16, 512], BF16) for i in range(2)]
            Y = [self.bank(es, f"o_y{i}") for i in range(4)]
            TR = [self.bank(es, f"o_tr{i}", BF16) for i in range(2)]
            pY = [Buf(psum=True) for _ in range(4)]
            pTR = [Buf(psum=True) for _ in range(2)]
            b_wo, b_gm, b_ss4, b_st, b_tmp, b_sqj, b_xn2 = (Buf() for _ in range(7))
            b_mixT = [Buf(), Buf()]
            b_xt = [Buf(), Buf()]
            b_x1 = [Buf(), Buf()]
            b_h2T = [Buf(), Buf()]
            b_dram = Buf()
            bc = self.b_const
            wsrc = self.w_out.rearrange("(kc p) n -> p kc n", p=128)
            for n in range(4):
                self.dma(POOL, wo[:, :, n * 512:(n + 1) * 512], wsrc[:, :, n * 512:(n + 1) * 512], writes=[b_wo])
            self.dma(SP, gm[:], self.GBC[0, :, :], writes=[b_gm])
            gi = 0
            for b in getattr(self, "out_blocks", range(4)):
                mi = b % 2
                c0 = b * 512
                self.dma(SP, mixT[mi][:], self.MIXT[:, :, c0:c0 + 512].rearrange("c p n -> p c n"), writes=[b_mixT[mi]])
                for t in range(4):
                    xi = gi % 2
                    gi += 1
                    r0 = NPRE + c0 + t * 128
                    self.dma(SP, xt[xi][:], self.xall[r0:r0 + 128, :], writes=[b_xt[xi]])
                    for n in range(4):
                        self.pre(PE, reads=[b_mixT[mi], b_wo], writes=[pY[n]])
                        for kc in range(16):
                            ins = nc.tensor.matmul(Y[n][:], lhsT=mixT[mi][:, kc, t * 128:(t + 1) * 128], rhs=wo[:, kc, n * 512:(n + 1) * 512], start=(kc == 0), stop=(kc == 15))
                        self.post(PE, ins, reads=[b_mixT[mi], b_wo], writes=[pY[n]])
                    self.op(DVE, lambda: nc.vector.memset(ss4[:], 0.0), writes=[b_ss4])
                    for n in range(4):
                        self.op(ACT, lambda: nc.scalar.activation(out=sqj[:, n * 512:(n + 1) * 512], in_=Y[n][:], func=AF.Square, accum_out=ss4[:, n:n + 1]),
                                reads=[pY[n], b_ss4], writes=[b_sqj, b_ss4])
                    self.op(DVE, lambda: nc.vector.reduce_sum(out=st[:, 0:1], in_=ss4[:, 0:4], axis=mybir.AxisListType.X), reads=[b_ss4], writes=[b_st])
                    self.op(DVE, lambda: nc.vector.tensor_scalar(out=st[:, 0:1], in0=st[:, 0:1], scalar1=1.0 / D, scalar2=EPS, op0=ALU.mult, op1=ALU.add), reads=[b_st], writes=[b_st])
                    self.rsqrt(st[:, 0:1], b_st)
                    for n in range(4):
                        cs_ = slice(n * 512, (n + 1) * 512)
                        self.op(DVE, lambda: nc.vector.scalar_tensor_tensor(out=tmp[:, cs_], in0=Y[n][:], scalar=st[:, 0:1], in1=gm[:, cs_], op0=ALU.mult, op1=ALU.mult),
                                reads=[pY[n], b_st, b_gm], writes=[b_tmp])
                    self.op(DVE, lambda: nc.vector.tensor_tensor(out=x1[xi][:], in0=tmp[:], in1=xt[xi][:], op=ALU.add), reads=[b_tmp, b_xt[xi]], writes=[b_x1[xi]])
                    self.dma(POOL, self.X1[c0 + t * 128:c0 + (t + 1) * 128, :], x1[xi][:], reads=[b_x1[xi]], writes=[b_dram])
                    self.op(DVE, lambda: nc.vector.memset(ss4[:, 4:5], 0.0), writes=[b_ss4])
                    self.op(ACT, lambda: nc.scalar.activation(out=sqj[:], in_=x1[xi][:], func=AF.Square, accum_out=ss4[:, 4:5]),
                            reads=[b_x1[xi], b_ss4], writes=[b_sqj, b_ss4])
                    self.op(DVE, lambda: nc.vector.tensor_scalar(out=st[:, 1:2], in0=ss4[:, 4:5], scalar1=1.0 / D, scalar2=EPS, op0=ALU.mult, op1=ALU.add), reads=[b_ss4], writes=[b_st])
                    self.rsqrt(st[:, 1:2], b_st)
                    self.op(DVE, lambda: nc.vector.tensor_scalar(out=xn2[:, t, :], in0=x1[xi][:], scalar1=st[:, 1:2], scalar2=None, op0=ALU.mult),
                            reads=[b_x1[xi], b_st], writes=[b_xn2])
                hi = b % 2
                for fc in range(16):
                    pi = fc % 2
                    self.pre(PE, reads=[b_xn2, bc], writes=[pTR[pi]])
                    for t in range(4):
                        ins = nc.tensor.transpose(out=TR[pi][:, t * 128:(t + 1) * 128], in_=xn2[:, t, fc * 128:(fc + 1) * 128], identity=self.ident_bf[:])
                    self.post(PE, ins, reads=[b_xn2, bc], writes=[pTR[pi]])
                    if fc % 2 == 0:
                        self.op(DVE, lambda: nc.vector.tensor_scalar(out=h2T[hi][:, fc, :], in0=TR[pi][:, 0:512], scalar1=self.Af[:, fc:fc + 1], scalar2=self.adaT[:, 48 + fc:49 + fc], op0=ALU.mult, op1=ALU.add),
                                reads=[pTR[pi], self.b_AB, self.b_ada], writes=[b_h2T[hi]])
                    else:
                        self.op(ACT, lambda: nc.scalar.activation(out=h2T[hi][:, fc, :], in_=TR[pi][:, 0:512], func=AF.Identity, scale=self.Af[:, fc:fc + 1], bias=self.adaT[:, 48 + fc:49 + fc]),
                                reads=[pTR[pi], self.b_AB, self.b_ada], writes=[b_h2T[hi]])
                self.dma(POOL, self.H2T[b], h2T[hi][:], reads=[b_h2T[hi]], writes=[b_dram])
            self.barrier()

    def phase_ffn(self):
        nc = self.nc
        PE, ACT, DVE, POOL, SP = self.PE, self.ACT, self.DVE, self.POOL, self.SP
        NJ = D_FF // 128
        with ExitStack() as es:
            h2T = self.sb(es, "f_h2T", [128, 16, 512], BF16)
            aT = self.sb(es, "f_aT", [128, NJ, 512], BF16)
            fT = self.sb(es, "f_fT", [128, 16, 512], F32)
            slab = [self.sb(es, f"f_slab{i}", [128, 4096], BF16) for i in range(4)]
            sg = [self.sb(es, f"f_sg{i}", [128, 512], F32) for i in range(2)]
            sqf = [self.sb(es, f"f_sqf{i}", [128, 512], BF16) for i in range(2)]
            rbc = self.sb(es, "f_rbc", [128, 512], F32)
            x1t = [self.sb(es, f"f_x1t{i}", [128, D], F32) for i in range(2)]
            ot = [self.sb(es, f"f_ot{i}", [128, D], F32) for i in range(2)]
            Fb = [self.bank(es, f"f_F{i}") for i in range(4)]
            SSb = self.bank(es, "f_SS")
            Tb = [self.bank(es, f"f_T{i}") for i in range(3)]
            pF = [Buf(psum=True) for _ in range(4)]
            pSS = Buf(psum=True)
            pT = [Buf(psum=True) for _ in range(3)]
            b_h2T, b_aT, b_fT, b_rbc = (Buf() for _ in range(4))
            b_slab = [Buf() for _ in range(4)]
            b_sg = [Buf(), Buf()]
            b_sqf = [Buf(), Buf()]
            b_x1t = [Buf(), Buf()]
            b_ot = [Buf(), Buf()]
            b_dram = Buf()
            bc = self.b_const
            wg3 = self.WG.rearrange("(kc p) n -> p kc n", p=128)
            wu3 = self.WU.rearrange("(kc p) n -> p kc n", p=128)
            wd3 = self.WD.rearrange("(j p) n -> p j n", p=128)
            sl_i = 0
            ti = 0
            for b in getattr(self, "ffn_blocks", range(4)):
                c0 = b * 512
                self.dma(SP, h2T[:], self.H2T[b], writes=[b_h2T])
                for g in range(NJ // 2):
                    sgi, sui = sl_i % 4, (sl_i + 1) % 4
                    sl_i += 2
                    wgs = slab[sgi][:].rearrange("p (k n) -> p k n", n=256)
                    wus = slab[sui][:].rearrange("p (k n) -> p k n", n=256)
                    self.dma(SP, wgs, wg3[:, :, g * 256:(g + 1) * 256], writes=[b_slab[sgi]])
                    self.dma(SP, wus, wu3[:, :, g * 256:(g + 1) * 256], writes=[b_slab[sui]])
                    for c in range(2):
                        j = 2 * g + c
                        gb_, ub_ = (0, 1) if j % 2 == 0 else (2, 3)
                        self.pre(PE, reads=[b_h2T, b_slab[sgi]], writes=[pF[gb_]])
                        for kc in range(16):
                            ins = nc.tensor.matmul(Fb[gb_][:], lhsT=wgs[:, kc, c * 128:(c + 1) * 128], rhs=h2T[:, kc, :], start=(kc == 0), stop=(kc == 15))
                        self.post(PE, ins, reads=[b_h2T, b_slab[sgi]], writes=[pF[gb_]])
                        self.pre(PE, reads=[b_h2T, b_slab[sui]], writes=[pF[ub_]])
                        for kc in range(16):
                            ins = nc.tensor.matmul(Fb[ub_][:], lhsT=wus[:, kc, c * 128:(c + 1) * 128], rhs=h2T[:, kc, :], start=(kc == 0), stop=(kc == 15))
                        self.post(PE, ins, reads=[b_h2T, b_slab[sui]], writes=[pF[ub_]])
                        si = j % 2
                        self.op(ACT, lambda: nc.scalar.activation(out=sg[si][:], in_=Fb[gb_][:], func=AF.Silu), reads=[pF[gb_]], writes=[b_sg[si]])
                        self.op(DVE, lambda: nc.vector.tensor_tensor(out=aT[:, j, :], in0=Fb[ub_][:], in1=sg[si][:], op=ALU.mult), reads=[pF[ub_], b_sg[si]], writes=[b_aT])
                groups = [(j0, min(NJ, j0 + 8)) for j0 in range(0, NJ, 8)]
                for qd in range(4):
                    for (j0, j1) in groups:
                        sdi = sl_i % 4
                        sl_i += 1
                        wds = slab[sdi][:].rearrange("p (j n) -> p j n", n=512)
                        self.dma(SP, wds[:, 0:j1 - j0, :], wd3[:, j0:j1, qd * 512:(qd + 1) * 512], writes=[b_slab[sdi]])
                        self.pre(PE, reads=[b_aT, b_slab[sdi]], writes=pF)
                        for j in range(j0, j1):
                            for dl in range(4):
                                ins = nc.tensor.matmul(Fb[dl][:], lhsT=wds[:, j - j0, dl * 128:(dl + 1) * 128], rhs=aT[:, j, :], start=(j == 0), stop=(j == NJ - 1))
                        self.post(PE, ins, reads=[b_aT, b_slab[sdi]], writes=pF)
                    for dl in range(4):
                        dch = qd * 4 + dl
                        self.op(ACT, lambda: nc.scalar.copy(out=fT[:, dch, :], in_=Fb[dl][:]), reads=[pF[dl]], writes=[b_fT])
                        qi = dch % 2
                        self.op(DVE, lambda: nc.vector.tensor_tensor(out=sqf[qi][:], in0=fT[:, dch, :], in1=fT[:, dch, :], op=ALU.mult), reads=[b_fT], writes=[b_sqf[qi]])
                        self.op(PE, lambda: nc.tensor.matmul(SSb[:], lhsT=self.ones_bf[:], rhs=sqf[qi][:], start=(dch == 0), stop=(dch == 15)),
                                reads=[b_sqf[qi], bc], writes=[pSS])
                self.op(DVE, lambda: nc.vector.tensor_scalar(out=rbc[:], in0=SSb[:], scalar1=1.0 / D, scalar2=EPS, op0=ALU.mult, op1=ALU.add), reads=[pSS], writes=[b_rbc])
                self.rsqrt(rbc[:], b_rbc)
                for dch in range(16):
                    self.op(DVE, lambda: nc.vector.scalar_tensor_tensor(out=fT[:, dch, :], in0=fT[:, dch, :], scalar=self.gfT[:, dch:dch + 1], in1=rbc[:], op0=ALU.mult, op1=ALU.mult),
                            reads=[b_fT, b_rbc, self.b_AB], writes=[b_fT])
                for t in range(4):
                    xi = ti % 2
                    ti += 1
                    self.dma(SP, x1t[xi][:], self.X1[c0 + t * 128:c0 + (t + 1) * 128, :], writes=[b_x1t[xi]])
                    for n in range(4):
                        tb_ = (t * 4 + n) % 3
                        self.pre(PE, reads=[b_fT, bc], writes=[pT[tb_]])
                        for dl in range(4):
                            ins = nc.tensor.transpose(out=Tb[tb_][:, dl * 128:(dl + 1) * 128], in_=fT[:, n * 4 + dl, t * 128:(t + 1) * 128], identity=self.ident_f[:])
                        self.post(PE, ins, reads=[b_fT, bc], writes=[pT[tb_]])
                        self.op(DVE, lambda: nc.vector.tensor_tensor(out=ot[xi][:, n * 512:(n + 1) * 512], in0=Tb[tb_][:], in1=x1t[xi][:, n * 512:(n + 1) * 512], op=ALU.add),
                                reads=[pT[tb_], b_x1t[xi]], writes=[b_ot[xi]])
                    self.dma(POOL, self.out[c0 + t * 128:c0 + (t + 1) * 128, :], ot[xi][:], reads=[b_ot[xi]], writes=[b_dram], is_output=True)
            self.barrier()


def _consts():
    c = {}
    c["c_ident_bf"] = np.eye(128, dtype=np.float32).astype(ml_dtypes.bfloat16)
    c["c_ident_f"] = np.eye(128, dtype=np.float32)
    misc = np.zeros((128, 8), np.float32)
    inv_freq = (1.0 / (10000.0 ** (np.arange(0, 64, 2, dtype=np.float32) / np.float32(64)))).astype(np.float32)
    p = np.arange(128)
    misc[:, 0] = inv_freq[p % 32]
    misc[:, 1] = np.where((p % 64) < 32, -1.0, 1.0)
    misc[:, 2] = -0.5
    c["c_misc"] = misc
    keep = np.ones((128, 512), np.float32)
    keep[:, ::64] = 0.0
    c["c_keep"] = keep
    c["c_tri_bf"] = (np.arange(128)[:, None] <= np.arange(128)[None, :]).astype(np.float32).astype(ml_dtypes.bfloat16)
    j = np.arange(128)[:, None]
    cc = np.arange(128)[None, :]
    c["c_mask128"] = ((j // 64 == cc // 64) & (j <= cc)).astype(np.float32)
    return c


def make_in_maps(inputs):
    x = np.asarray(inputs["x"])
    pos = np.asarray(inputs["positions"]).astype(np.int32)
    cc = np.asarray(inputs["c"])
    consts = _consts()
    shared = {}
    for k in ("w_ada", "w_in", "w_uq", "w_uk", "w_uv", "w_gate_up", "w_out", "w_ffn_gate", "w_ffn_up", "w_ffn_down"):
        shared[k] = np.ascontiguousarray(np.asarray(inputs[k])[0])
    for k in ("b_ada", "g_post_mix", "g_post_ffn"):
        shared[k] = np.ascontiguousarray(np.asarray(inputs[k])[0][None, :])
    for k in ("g_pre_mix", "g_q", "g_kv", "b_gate", "g_gla", "g_pre_ffn", "g_post_ffn"):
        v = np.asarray(inputs[k])[0]
        shared[k + "T"] = np.ascontiguousarray(v.reshape(-1, 128).T)
    maps = []
    for core in range(8):
        b, q = core // 4, core % 4
        quarters, valid = [], []
        for s in range(3):
            qs = s - (3 - q)
            valid.append(qs >= 0)
            quarters.append(max(qs, 0))
        quarters.append(q)
        xall = np.concatenate([x[b, 2048 * k:2048 * (k + 1)] for k in quarters], 0)
        posall = np.concatenate([pos[b, 2048 * k:2048 * (k + 1)] for k in quarters], 0)[None, :]
        sm = np.zeros((128, 8), np.float32)
        for s in range(3):
            sm[:, s] = 0.0 if valid[s] else NEG
            sm[:, 4 + s] = 1.0 if valid[s] else 0.0
        m = dict(shared)
        m.update(consts)
        m["xall"] = np.ascontiguousarray(xall)
        m["posall"] = np.ascontiguousarray(posall)
        m["cvec"] = np.ascontiguousarray(cc[b].reshape(16, 128).T)
        m["slotmask"] = sm
        maps.append(m)
    return maps


_CACHE = {}


def kernel(**inputs):
    maps = make_in_maps(inputs)
    if "nc" not in _CACHE:
        _CACHE["nc"] = Prog().build()
    res = run_bass_kernel_spmd(_CACHE["nc"], maps, core_ids=list(range(8)))
    out = np.zeros((2, 8192, D), np.float32)
    for core in range(8):
        b, q = core // 4, core % 4
        out[b, 2048 * q:2048 * (q + 1)] = np.asarray(res.results[core]["out"])
    return out
```

```python
import numpy as np
import ml_dtypes
from contextlib import ExitStack
import concourse.bass as bass
import concourse.mybir as mybir
from concourse.bass_utils import run_bass_kernel_spmd

F32 = mybir.dt.float32
BF16 = mybir.dt.bfloat16
I32 = mybir.dt.int32
AF = mybir.ActivationFunctionType
ALU = mybir.AluOpType

D = 2048
NT = 8192
NOWN = 2048
NPRE = 6144
NB = 16
NBP = 12
EPS = 1e-6
IN_COLS = 3920
C_Q, C_KV, C_KR, C_QG, C_KG, C_VG, C_AG, C_RG = 0, 512, 768, 832, 1344, 1856, 2880, 2896
D_FF = 5632
NEG = -30000.0
TWO_PI = 2.0 * np.pi
CW1 = 6.28125
CW2 = float(TWO_PI - 6.28125)


class Buf:
    __slots__ = ("w", "r", "name", "psum")

    def __init__(self, name="", psum=False):
        self.w = None
        self.r = {}
        self.name = name
        self.psum = psum


class DSem:
    __slots__ = ("sem", "cnt")

    def __init__(self, sem):
        self.sem = sem
        self.cnt = 0


class Queue:
    def __init__(self, eng, sem, is_pe=False, name=""):
        self.eng = eng
        self.sem = sem
        self.cnt = 0
        self.seen = {}
        self.own_seen = 0
        self.is_pe = is_pe
        self.dsems = []
        self.dk = 0
        self.name = name

    def wait(self, tok):
        s, v = tok
        key = id(s)
        if self.seen.get(key, 0) >= v:
            return
        self.eng.wait_ge(s, v)
        self.seen[key] = v

    def own_wait(self, v):
        if self.own_seen >= v:
            return
        self.eng.wait_ge(self.sem, v)
        self.own_seen = v


class Builder:
    def __init__(self):
        self.nc = bass.Bass("TRN2", target_bir_lowering=False)
        nc = self.nc
        self.es = ExitStack()
        self.PE = Queue(nc.tensor, self._sem("q_pe"), is_pe=True, name="pe")
        self.ACT = Queue(nc.scalar, self._sem("q_act"), name="act")
        self.DVE = Queue(nc.vector, self._sem("q_dve"), name="dve")
        self.POOL = Queue(nc.gpsimd, self._sem("q_pool"), name="pool")
        self.SP = Queue(nc.sync, self._sem("q_sp"), name="sp")
        self.queues = [self.PE, self.ACT, self.DVE, self.POOL, self.SP]
        for q, n in ((self.SP, 8), (self.POOL, 8), (self.ACT, 4)):
            q.dsems = [DSem(self._sem(f"d_{q.name}{i}")) for i in range(n)]
        self.out_toks = []
        self.n_inst = 0

    def _sem(self, name):
        return self.es.enter_context(self.nc.semaphore(name))

    def pre(self, q, reads=(), writes=()):
        for b in reads:
            t = b.w
            if t is not None:
                if t[0] is q.sem:
                    if not q.is_pe:
                        q.own_wait(t[1])
                else:
                    q.wait(t)
        for b in writes:
            t = b.w
            if t is not None:
                if t[0] is q.sem:
                    if not q.is_pe:
                        q.own_wait(t[1])
                else:
                    q.wait(t)
            for (s, v) in b.r.values():
                if s is q.sem:
                    if not q.is_pe:
                        q.own_wait(v)
                else:
                    q.wait((s, v))

    def post(self, q, ins, reads=(), writes=()):
        q.cnt += 1
        ins.then_inc(q.sem, 1)
        tok = (q.sem, q.cnt)
        self._reg(tok, reads, writes)
        self.n_inst += 1

    def _reg(self, tok, reads, writes):
        for b in reads:
            if b.psum:
                b.w = tok
            else:
                b.r[id(tok[0])] = tok
        for b in writes:
            b.w = tok
            b.r = {}

    def op(self, q, fn, reads=(), writes=()):
        self.pre(q, reads, writes)
        ins = fn()
        self.post(q, ins, reads, writes)
        return ins

    def dma(self, q, out, in_, reads=(), writes=(), is_output=False, **kw):
        self.pre(q, reads, writes)
        ds = q.dsems[q.dk % len(q.dsems)]
        q.dk += 1
        if ds.cnt > 0:
            q.wait((ds.sem, ds.cnt * 16))
        ins = q.eng.dma_start(out=out, in_=in_, **kw)
        ds.cnt += 1
        ins.then_inc(ds.sem, 16)
        tok = (ds.sem, ds.cnt * 16)
        self._reg(tok, reads, writes)
        if is_output:
            self.out_toks.append(tok)
        self.n_inst += 1
        return tok

    def barrier(self):
        toks = []
        for q in self.queues:
            if q.cnt > 0:
                toks.append((q.sem, q.cnt))
            for ds in q.dsems:
                if ds.cnt > 0:
                    toks.append((ds.sem, ds.cnt * 16))
        for q in self.queues:
            for t in toks:
                if t[0] is q.sem:
                    continue
                q.wait(t)

    def finish(self):
        self.barrier()


class Prog(Builder):
    def __init__(self, debug=None, phases=("P0", "A1", "G", "ATT", "OUT", "FFN")):
        super().__init__()
        self.debug = debug or set()
        self.phases = tuple(phases)
        self.in_names = []
        nc = self.nc
        dt = nc.dram_tensor

        def din(name, shape, dtype=F32, ph=None):
            if ph is not None and not (set(ph) & set(self.phases)):
                return None
            self.in_names.append(name)
            return dt(name, list(shape), dtype, kind="ExternalInput").ap()

        def dscr(name, shape, dtype):
            kind = "ExternalOutput" if name in self.debug else "Internal"
            return dt(name, list(shape), dtype, kind=kind).ap()

        self.xall = din("xall", [NT, D])
        self.posall = din("posall", [1, NT], I32)
        self.cvec = din("cvec", [128, 16])
        self.slotmask = din("slotmask", [128, 8])
        self.w_ada = din("w_ada", [D, 6 * D], ph=("P0",))
        self.b_ada_in = din("b_ada", [1, 6 * D])
        self.g_pre_mixT = din("g_pre_mixT", [128, 16])
        self.g_post_mix = din("g_post_mix", [1, D])
        self.w_in = din("w_in", [D, IN_COLS])
        self.g_qT = din("g_qT", [128, 4])
        self.w_uq = din("w_uq", [512, 1536], ph=("ATT",))
        self.g_kvT = din("g_kvT", [128, 2])
        self.w_uk = din("w_uk", [256, 1024], ph=("ATT",))
        self.w_uv = din("w_uv", [256, 1024], ph=("ATT",))
        self.w_gate_up = din("w_gate_up", [16, 512])
        self.b_gateT = din("b_gateT", [128, 4])
        self.g_glaT = din("g_glaT", [128, 2])
        self.w_out = din("w_out", [D, D], ph=("OUT",))
        self.g_pre_ffnT = din("g_pre_ffnT", [128, 16])
        self.g_post_ffn = din("g_post_ffn", [1, D])
        self.g_post_ffnT = din("g_post_ffnT", [128, 16])
        self.w_ffn_gate = din("w_ffn_gate", [D, D_FF], ph=("FFN",))
        self.w_ffn_up = din("w_ffn_up", [D, D_FF], ph=("FFN",))
        self.w_ffn_down = din("w_ffn_down", [D_FF, D], ph=("FFN",))
        self.c_ident_bf = din("c_ident_bf", [128, 128], BF16)
        self.c_ident_f = din("c_ident_f", [128, 128])
        self.c_misc = din("c_misc", [128, 8])
        self.c_keep = din("c_keep", [128, 512])
        self.c_tri_bf = din("c_tri_bf", [128, 128], BF16)
        self.c_mask128 = din("c_mask128", [128, 128])
        self.out = dt("out", [NOWN, D], F32, kind="ExternalOutput").ap()
        self.HT = dscr("HT", [NB, 128, 16, 512], BF16)
        self.CKVN = dscr("CKVN", [2, 128, NT], BF16)
        self.KROPE = dscr("KROPE", [64, NT], BF16)
        self.CQN = dscr("CQN", [4, 128, NOWN], BF16)
        self.MIXT = dscr("MIXT", [16, 128, NOWN], BF16)
        self.ADAT = dscr("ADAT", [128, 96], F32)
        self.ROPE = dscr("ROPE", [2, 64, NT], F32)
        self.GBC = dscr("GBC", [2, 128, D], F32)
        self.X1 = dscr("X1", [NOWN, D], F32)
        self.H2T = dscr("H2T", [4, 128, 16, 512], BF16)
        self.WG = dscr("WG", [D, D_FF], BF16)
        self.WU = dscr("WU", [D, D_FF], BF16)
        self.WD = dscr("WD", [D_FF, D], BF16)

    def rsqrt(self, ap, buf):
        nc = self.nc
        self.op(self.ACT, lambda: nc.scalar.activation(out=ap, in_=ap, func=AF.Ln), reads=[buf], writes=[buf])
        self.op(self.ACT, lambda: nc.scalar.activation(out=ap, in_=ap, func=AF.Exp, scale=-0.5), reads=[buf], writes=[buf])

    def sb(self, es, name, shape, dtype):
        return es.enter_context(self.nc.sbuf_tensor(name, list(shape), dtype))

    def ps(self, es, name, shape, dtype):
        return es.enter_context(self.nc.psum_tensor(name, list(shape), dtype))

    def bank(self, es, name, dtype=F32):
        n = 512 if dtype == F32 else 1024
        return es.enter_context(self.nc.psum_tensor(name, [128, n], dtype))

    def build(self):
        nc = self.nc
        phases = self.phases
        PE, ACT, DVE, POOL, SP = self.PE, self.ACT, self.DVE, self.POOL, self.SP
        es0 = self.es
        self.ident_bf = self.sb(es0, "ident_bf", [128, 128], BF16)
        self.ident_f = self.sb(es0, "ident_f", [128, 128], F32)
        self.ones_bf = self.sb(es0, "ones_bf", [128, 128], BF16)
        self.ones_f = self.sb(es0, "ones_f", [128, 128], F32)
        self.misc = self.sb(es0, "misc", [128, 8], F32)
        self.smask = self.sb(es0, "smask", [128, 8], F32)
        self.adaT = self.sb(es0, "adaT", [128, 96], F32)
        self.Am = self.sb(es0, "Am", [128, 16], F32)
        self.Af = self.sb(es0, "Af", [128, 16], F32)
        self.gpreT = self.sb(es0, "gpreT", [128, 32], F32)
        self.gqT = self.sb(es0, "gqT", [128, 4], F32)
        self.gkvT = self.sb(es0, "gkvT", [128, 2], F32)
        self.gfT = self.sb(es0, "gfT", [128, 16], F32)
        self.negb = self.sb(es0, "negb", [128, 4], F32)
        self.gglaT = self.sb(es0, "gglaT", [128, 2], F32)
        self.b_gbc = Buf("gbc_dram")
        self.b_const = Buf("const")
        self.b_ada = Buf("adaT")
        self.b_gm = Buf("gm")
        self.b_gf = Buf("gf")
        self.b_AB = Buf("AB")

        def ld(q, dst, src, b):
            self.dma(q, dst, src, writes=[b])

        bc = self.b_const
        ld(SP, self.ident_bf[:], self.c_ident_bf[:, :], bc)
        ld(SP, self.ident_f[:], self.c_ident_f[:, :], bc)
        ld(SP, self.misc[:], self.c_misc[:, :], bc)
        ld(SP, self.smask[:], self.slotmask[:, :], bc)
        self.dma(SP, self.gpreT[:, 0:16], self.g_pre_mixT[:, :], writes=[bc])
        self.dma(SP, self.gpreT[:, 16:32], self.g_pre_ffnT[:, :], writes=[bc])
        self.dma(SP, self.gqT[:], self.g_qT[:, :], writes=[bc])
        self.dma(SP, self.gkvT[:], self.g_kvT[:, :], writes=[bc])
        self.dma(SP, self.gfT[:], self.g_post_ffnT[:, :], writes=[bc])
        self.dma(SP, self.negb[:], self.b_gateT[:, :], writes=[bc])
        self.dma(SP, self.gglaT[:], self.g_glaT[:, :], writes=[bc])
        self.op(DVE, lambda: nc.vector.tensor_scalar(out=self.negb[:], in0=self.negb[:], scalar1=-1.0, scalar2=None, op0=ALU.mult), reads=[bc], writes=[bc])
        self.op(DVE, lambda: nc.vector.memset(self.ones_f[:], 1.0), writes=[bc])
        self.op(DVE, lambda: nc.vector.memset(self.ones_bf[:], 1.0), writes=[bc])

        if "P0" in phases:
            self.phase_p0()
        if "A1" in phases:
            self.barrier()
            self.phase_a1()
        if "G" in phases:
            self.barrier()
            self.phase_g()
        if "ATT" in phases:
            self.barrier()
            self.phase_att()
        if "OUT" in phases:
            self.barrier()
            self.phase_out()
        if "FFN" in phases:
            self.barrier()
            self.phase_ffn()
        self.finish()
        return nc

    def phase_p0(self):
        nc = self.nc
        PE, ACT, DVE, POOL, SP = self.PE, self.ACT, self.DVE, self.POOL, self.SP
        with ExitStack() as es:
            cv = self.sb(es, "p0_cv", [128, 16], F32)
            scb = self.sb(es, "p0_scb", [128, 16], BF16)
            wa = [self.sb(es, f"p0_wa{i}", [128, 16, 512], BF16) for i in range(2)]
            brow = [self.sb(es, f"p0_brow{i}", [1, 512], F32) for i in range(2)]
            row = [self.sb(es, f"p0_row{i}", [1, 512], F32) for i in range(2)]
            gpost = self.sb(es, "p0_gpost", [1, 2, D], F32)
            self.gm_bc = self.sb(es, "gm_bc", [128, D], F32)
            self.gf_bc = self.sb(es, "gf_bc", [128, D], F32)
            row2 = [self.sb(es, f"p0_row2{i}", [1, 512], F32) for i in range(2)]
            b_row2 = [Buf(), Buf()]
            prow_b = [self.bank(es, f"p0_prow{i}") for i in range(2)]
            pcol_b = [self.bank(es, f"p0_pcol{i}") for i in range(2)]
            pbc = [self.bank(es, f"p0_pbc{i}") for i in range(2)]
            prow = [t[0:1, :] for t in prow_b]
            pcol = [t[:, 0:4] for t in pcol_b]
            b_cv, b_scb, b_gpost = Buf(), Buf(), Buf()
            b_wa = [Buf(), Buf()]
            b_brow = [Buf(), Buf()]
            b_row = [Buf(), Buf()]
            b_prow = [Buf(psum=True), Buf(psum=True)]
            b_pcol = [Buf(psum=True), Buf(psum=True)]
            b_pbc = [Buf(psum=True), Buf(psum=True)]
            self.dma(SP, cv[:], self.cvec[:, :], writes=[b_cv])
            self.dma(SP, gpost[:, 0, :], self.g_post_mix[0:1, :], writes=[b_gpost])
            self.dma(SP, gpost[:, 1, :], self.g_post_ffn[0:1, :], writes=[b_gpost])
            self.op(ACT, lambda: nc.scalar.activation(out=scb[:], in_=cv[:], func=AF.Silu), reads=[b_cv], writes=[b_scb])
            wsrc = self.w_ada.rearrange("(kc p) n -> p kc n", p=128)
            for nb in range(24):
                i = nb % 2
                self.dma(POOL, wa[i][:], wsrc[:, :, nb * 512:(nb + 1) * 512], writes=[b_wa[i]])
                self.dma(SP, brow[i][:], self.b_ada_in[0:1, nb * 512:(nb + 1) * 512], writes=[b_brow[i]])
                self.pre(PE, reads=[b_scb, b_wa[i]], writes=[b_prow[i]])
                for kc in range(16):
                    ins = nc.tensor.matmul(prow[i], lhsT=scb[:, kc:kc + 1], rhs=wa[i][:, kc, :], start=(kc == 0), stop=(kc == 15))
                self.post(PE, ins, reads=[b_scb, b_wa[i]], writes=[b_prow[i]])
                self.op(DVE, lambda: nc.vector.tensor_tensor(out=row[i][:], in0=prow[i], in1=brow[i][:], op=ALU.add),
                        reads=[b_prow[i], b_brow[i]], writes=[b_row[i]])
                self.pre(PE, reads=[b_row[i], self.b_const], writes=[b_pcol[i]])
                for j in range(4):
                    ins = nc.tensor.matmul(pcol[i][:, j:j + 1], lhsT=row[i][0:1, j * 128:(j + 1) * 128], rhs=self.ones_f[0:1, 0:1], start=True, stop=True)
                self.post(PE, ins, reads=[b_row[i], self.b_const], writes=[b_pcol[i]])
                self.op(DVE, lambda: nc.vector.tensor_copy(out=self.adaT[:, nb * 4:nb * 4 + 4], in_=pcol[i]),
                        reads=[b_pcol[i]], writes=[self.b_ada])
                g = nb // 4
                if g in (2, 5):
                    dst, bb, gi = (self.gm_bc, self.b_gm, 0) if g == 2 else (self.gf_bc, self.b_gf, 1)
                    c0 = (nb % 4) * 512
                    self.op(DVE, lambda: nc.vector.tensor_tensor(out=row2[i][:], in0=row[i][:], in1=gpost[0:1, gi, c0:c0 + 512], op=ALU.mult),
                            reads=[b_row[i], b_gpost], writes=[b_row2[i]])
                    self.op(PE, lambda: nc.tensor.matmul(pbc[i][:], lhsT=self.ones_f[0:1, :], rhs=row2[i][:], start=True, stop=True),
                            reads=[b_row2[i], self.b_const], writes=[b_pbc[i]])
                    self.op(DVE, lambda: nc.vector.tensor_copy(out=dst[:, c0:c0 + 512], in_=pbc[i][:]),
                            reads=[b_pbc[i]], writes=[bb])
            self.op(DVE, lambda: nc.vector.scalar_tensor_tensor(out=self.Am[:], in0=self.adaT[:, 16:32], scalar=1.0, in1=self.gpreT[:, 0:16], op0=ALU.add, op1=ALU.mult),
                    reads=[self.b_ada, self.b_const], writes=[self.b_AB])
            self.op(DVE, lambda: nc.vector.scalar_tensor_tensor(out=self.Af[:], in0=self.adaT[:, 64:80], scalar=1.0, in1=self.gpreT[:, 16:32], op0=ALU.add, op1=ALU.mult),
                    reads=[self.b_ada, self.b_const], writes=[self.b_AB])
            self.op(DVE, lambda: nc.vector.tensor_tensor(out=self.gfT[:], in0=self.gfT[:], in1=self.adaT[:, 80:96], op=ALU.mult),
                    reads=[self.b_ada, self.b_const], writes=[self.b_AB])
            self.dma(POOL, self.GBC[0, :, :], self.gm_bc[:], reads=[self.b_gm], writes=[self.b_gbc])
            self.dma(POOL, self.GBC[1, :, :], self.gf_bc[:], reads=[self.b_gf], writes=[self.b_gbc])
            if "ADAT" in self.debug:
                self.dma(POOL, self.ADAT[:, :], self.adaT[:], reads=[self.b_ada], is_output=True)
            self.barrier()

    def phase_a1(self):
        nc = self.nc
        PE, ACT, DVE, POOL, SP = self.PE, self.ACT, self.DVE, self.POOL, self.SP
        NCOL = 512 + 256 + 128
        with ExitStack() as es:
            win = self.sb(es, "a1_win", [128, 16, NCOL], BF16)
            xt = [self.sb(es, f"a1_xt{i}", [128, D], F32) for i in range(4)]
            sq = self.sb(es, "a1_sq", [128, D], BF16)
            ss = self.sb(es, "a1_ss", [128, 8], F32)
            rs = self.sb(es, "a1_rs", [128, 8], F32)
            xn = [self.sb(es, f"a1_xn{i}", [128, 4, D], BF16) for i in range(2)]
            hT = [self.sb(es, f"a1_hT{i}", [128, 16, 512], BF16) for i in range(2)]
            posi = self.sb(es, "a1_posi", [64, 512], I32)
            posf = self.sb(es, "a1_posf", [64, 512], F32)
            ang = self.sb(es, "a1_ang", [64, 512], F32)
            rt = self.sb(es, "a1_rt", [64, 2, 512], F32)
            rti = self.sb(es, "a1_rti", [64, 2, 512], I32)
            rope = [self.sb(es, f"a1_rope{i}", [64, 2, 512], F32) for i in range(3)]
            raw = self.sb(es, "a1_raw", [128, 4, 512], F32)
            sqz = self.sb(es, "a1_sqz", [128, 4, 512], BF16)
            rbc = self.sb(es, "a1_rbc", [128, 512], F32)
            stg = [self.sb(es, f"a1_stg{i}", [128, 4, 512], BF16) for i in range(2)]
            kt1 = self.sb(es, "a1_kt1", [64, 512], F32)
            kt2 = self.sb(es, "a1_kt2", [64, 512], F32)
            krs = [self.sb(es, f"a1_krs{i}", [64, 512], BF16) for i in range(2)]
            ptr = [self.bank(es, f"a1_ptr{i}", BF16) for i in range(2)]
            pz = [self.bank(es, f"a1_pz{i}") for i in range(4)]
            pss = self.bank(es, "a1_pss")

            b_win = Buf("win")
            b_xt = [Buf() for _ in range(4)]
            b_sq, b_ss, b_rs = Buf(), Buf(), Buf()
            b_xn = [Buf(), Buf()]
            b_hT = [[Buf() for _ in range(16)] for _ in range(2)]
            b_posi, b_posf, b_ang, b_rt, b_rti = Buf(), Buf(), Buf(), Buf(), Buf()
            b_rope = [Buf(), Buf(), Buf()]
            b_raw, b_sqz, b_rbc = Buf(), Buf(), Buf()
            b_stg = [Buf(), Buf()]
            b_kt1, b_kt2 = Buf(), Buf()
            b_krs = [Buf(), Buf()]
            b_ptr = [Buf(psum=True), Buf(psum=True)]
            b_pz = [Buf(psum=True) for _ in range(4)]
            b_pss = Buf(psum=True)
            b_dram = Buf("a1_dram")

            wsrc = self.w_in.rearrange("(kc p) n -> p kc n", p=128)
            self.dma(POOL, win[:, :, 0:832], wsrc[:, :, 0:832], writes=[b_win])
            self.dma(POOL, win[:, :, 832:864], wsrc[:, :, 800:832], writes=[b_win])
            self.dma(POOL, win[:, :, 864:896], wsrc[:, :, 768:800], writes=[b_win])

            pz_i = 0
            import os as _os
            STOP = float(_os.environ.get("A1_STOP", "99"))
            def front(tb):
                nonlocal pz_i
                own = tb >= NBP
                hb = tb % 2
                t0 = tb * 512
                rb = tb % 3
                self.dma(SP, posi[:], self.posall[0:1, t0:t0 + 512].to_broadcast([64, 512]), writes=[b_posi])
                self.op(DVE, lambda: nc.vector.tensor_copy(out=posf[:], in_=posi[:]), reads=[b_posi], writes=[b_posf])
                self.op(DVE, lambda: nc.vector.tensor_scalar(out=ang[:], in0=posf[:], scalar1=self.misc[0:64, 0:1], scalar2=None, op0=ALU.mult),
                        reads=[b_posf, self.b_const], writes=[b_ang])
                self.op(DVE, lambda: nc.vector.tensor_scalar(out=rt[:, 0, :], in0=ang[:], scalar1=float(1.0 / TWO_PI), scalar2=0.25, op0=ALU.mult, op1=ALU.add),
                        reads=[b_ang], writes=[b_rt])
                self.op(DVE, lambda: nc.vector.tensor_scalar(out=rt[:, 1, :], in0=ang[:], scalar1=float(1.0 / TWO_PI), scalar2=None, op0=ALU.mult),
                        reads=[b_ang], writes=[b_rt])
                self.op(DVE, lambda: nc.vector.tensor_copy(out=rti[:], in_=rt[:]), reads=[b_rt], writes=[b_rti])
                self.op(DVE, lambda: nc.vector.tensor_copy(out=rt[:], in_=rti[:]), reads=[b_rti], writes=[b_rt])
                for k in range(2):
                    self.op(DVE, lambda: nc.vector.scalar_tensor_tensor(out=rope[rb][:, k, :], in0=rt[:, k, :], scalar=-CW1, in1=ang[:], op0=ALU.mult, op1=ALU.add),
                            reads=[b_rt, b_ang], writes=[b_rope[rb]])
                    self.op(DVE, lambda: nc.vector.scalar_tensor_tensor(out=rope[rb][:, k, :], in0=rt[:, k, :], scalar=-CW2, in1=rope[rb][:, k, :], op0=ALU.mult, op1=ALU.add),
                            reads=[b_rt, b_rope[rb]], writes=[b_rope[rb]])
                self.op(DVE, lambda: nc.vector.tensor_scalar(out=rope[rb][:, 0, :], in0=rope[rb][:, 0, :], scalar1=float(np.pi / 2), scalar2=None, op0=ALU.add),
                        reads=[b_rope[rb]], writes=[b_rope[rb]])
                self.op(DVE, lambda: nc.vector.tensor_scalar(out=rope[rb][:], in0=rope[rb][:], scalar1=float(-np.pi), scalar2=float(np.pi), op0=ALU.max, op1=ALU.min),
                        reads=[b_rope[rb]], writes=[b_rope[rb]])
                self.op(ACT, lambda: nc.scalar.activation(out=rope[rb][:, 0, :], in_=rope[rb][:, 0, :], func=AF.Sin),
                        reads=[b_rope[rb]], writes=[b_rope[rb]])
                self.op(ACT, lambda: nc.scalar.activation(out=rope[rb][:, 1, :], in_=rope[rb][:, 1, :], func=AF.Sin, scale=self.misc[0:64, 1:2]),
                        reads=[b_rope[rb], self.b_const], writes=[b_rope[rb]])
                self.dma(POOL, self.ROPE[:, :, t0:t0 + 512].rearrange("k p n -> p k n"), rope[rb][:], reads=[b_rope[rb]], writes=[b_dram],
                         is_output=("ROPE" in self.debug))

                if STOP <= 1:
                    return
                so = (tb % 2) * 4
                self.op(DVE, lambda: nc.vector.memset(ss[:, so:so + 4], 0.0), writes=[b_ss])
                for t in range(4):
                    self.dma(SP, xt[t][:], self.xall[t0 + t * 128:t0 + (t + 1) * 128, :], writes=[b_xt[t]])
                    self.op(ACT, lambda: nc.scalar.activation(out=sq[:], in_=xt[t][:], func=AF.Square, accum_out=ss[:, so + t:so + t + 1]),
                            reads=[b_xt[t], b_ss], writes=[b_sq, b_ss])
                self.op(DVE, lambda: nc.vector.tensor_scalar(out=rs[:, so:so + 4], in0=ss[:, so:so + 4], scalar1=1.0 / D, scalar2=EPS, op0=ALU.mult, op1=ALU.add),
                        reads=[b_ss], writes=[b_rs])
                self.rsqrt(rs[:, so:so + 4], b_rs)
                for t in range(4):
                    self.op(DVE, lambda: nc.vector.tensor_scalar(out=xn[hb][:, t, :], in0=xt[t][:], scalar1=rs[:, so + t:so + t + 1], scalar2=None, op0=ALU.mult),
                            reads=[b_xt[t], b_rs], writes=[b_xn[hb]])

            def backT(tb):
                nonlocal pz_i
                own = tb >= NBP
                hb = tb % 2
                t0 = tb * 512
                rb = tb % 3
                for fc in range(16):
                    pi = fc % 2
                    self.pre(PE, reads=[b_xn[hb], self.b_const], writes=[b_ptr[pi]])
                    for t in range(4):
                        ins = nc.tensor.transpose(out=ptr[pi][:, t * 128:(t + 1) * 128], in_=xn[hb][:, t, fc * 128:(fc + 1) * 128], identity=self.ident_bf[:])
                    self.post(PE, ins, reads=[b_xn[hb], self.b_const], writes=[b_ptr[pi]])
                    if STOP <= 3.2:
                        continue
                    if fc % 2 == 0:
                        self.op(DVE, lambda: nc.vector.tensor_scalar(out=hT[hb][:, fc, :], in0=ptr[pi][:, 0:512], scalar1=self.Am[:, fc:fc + 1], scalar2=self.adaT[:, fc:fc + 1], op0=ALU.mult, op1=ALU.add),
                                reads=[b_ptr[pi], self.b_AB, self.b_ada], writes=[b_hT[hb][fc]])
                    else:
                        self.op(ACT, lambda: nc.scalar.activation(out=hT[hb][:, fc, :], in_=ptr[pi][:, 0:512], func=AF.Identity, scale=self.Am[:, fc:fc + 1], bias=self.adaT[:, fc:fc + 1]),
                                reads=[b_ptr[pi], self.b_AB, self.b_ada], writes=[b_hT[hb][fc]])
                if STOP <= 3.5:
                    return
                _n = int(_os.environ.get("HTN", "16"))
                self.dma(POOL, self.HT[tb, :, 0:_n, :], hT[hb][:, 0:_n, :], reads=b_hT[hb], writes=[b_dram], is_output=("HT" in self.debug))

            def backP(tb):
                nonlocal pz_i
                own = tb >= NBP
                hb = tb % 2
                t0 = tb * 512
                rb = tb % 3
                if STOP <= 3:
                    return
                def proj_norm(col0, nch, gT, dst_ap_fn, inv_n):
                    nonlocal pz_i
                    for mc in range(nch):
                        p = pz_i % 4
                        pz_i += 1
                        self.pre(PE, reads=b_hT[hb] + [b_win], writes=[b_pz[p]])
                        for kc in range(16):
                            ins = nc.tensor.matmul(pz[p][:], lhsT=win[:, kc, col0 + mc * 128:col0 + (mc + 1) * 128], rhs=hT[hb][:, kc, :], start=(kc == 0), stop=(kc == 15))
                        self.post(PE, ins, reads=b_hT[hb] + [b_win], writes=[b_pz[p]])
                        if STOP <= 3.55:
                            continue
                        self.op(ACT, lambda: nc.scalar.activation(out=sqz[:, mc, :], in_=pz[p][:], func=AF.Square), reads=[b_pz[p]], writes=[b_sqz])
                        if STOP <= 3.57:
                            continue
                        self.op(DVE, lambda: nc.vector.tensor_scalar(out=raw[:, mc, :], in0=pz[p][:], scalar1=gT[:, mc:mc + 1], scalar2=None, op0=ALU.mult),
                                reads=[b_pz[p], self.b_const], writes=[b_raw])
                    if STOP <= 3.6:
                        return
                    self.pre(PE, reads=[b_sqz, self.b_const], writes=[b_pss])
                    for mc in range(nch):
                        ins = nc.tensor.matmul(pss[:], lhsT=self.ones_bf[:], rhs=sqz[:, mc, :], start=(mc == 0), stop=(mc == nch - 1))
                    self.post(PE, ins, reads=[b_sqz, self.b_const], writes=[b_pss])
                    self.op(DVE, lambda: nc.vector.tensor_scalar(out=rbc[:], in0=pss[:], scalar1=inv_n, scalar2=EPS, op0=ALU.mult, op1=ALU.add),
                            reads=[b_pss], writes=[b_rbc])
                    if STOP <= 3.7:
                        return
                    self.rsqrt(rbc[:], b_rbc)
                    if STOP <= 3.8:
                        return
                    si = tb % 2
                    for mc in range(nch):
                        self.op(DVE, lambda: nc.vector.tensor_tensor(out=stg[si][:, mc, :], in0=raw[:, mc, :], in1=rbc[:], op=ALU.mult),
                                reads=[b_raw, b_rbc], writes=[b_stg[si]])
                    if STOP <= 3.9:
                        return
                    dst_ap_fn(stg[si], b_stg[si])

                def st_ckv(st, bst):
                    self.dma(POOL, self.CKVN[:, :, t0:t0 + 512].rearrange("c p n -> p c n"), st[:, 0:2, :], reads=[bst], writes=[b_dram],
                             is_output=("CKVN" in self.debug))

                proj_norm(512, 2, self.gkvT, st_ckv, 1.0 / 256)
                if own:
                    o0 = t0 - NPRE

                    def st_cq(st, bst):
                        self.dma(POOL, self.CQN[:, :, o0:o0 + 512].rearrange("c p n -> p c n"), st[:, 0:4, :], reads=[bst], writes=[b_dram],
                                 is_output=("CQN" in self.debug))
                    proj_norm(0, 4, self.gqT, st_cq, 1.0 / 512)
                if STOP <= 4:
                    return
                pa = pz_i % 4
                pb = (pz_i + 1) % 4
                pz_i += 2
                for (p, c0) in ((pa, 768), (pb, 832)):
                    self.pre(PE, reads=b_hT[hb] + [b_win], writes=[b_pz[p]])
                    for kc in range(16):
                        ins = nc.tensor.matmul(pz[p][0:64, :], lhsT=win[:, kc, c0:c0 + 64], rhs=hT[hb][:, kc, :], start=(kc == 0), stop=(kc == 15))
                    self.post(PE, ins, reads=b_hT[hb] + [b_win], writes=[b_pz[p]])
                self.op(DVE, lambda: nc.vector.tensor_tensor(out=kt1[:], in0=pz[pa][0:64, :], in1=rope[rb][:, 0, :], op=ALU.mult),
                        reads=[b_pz[pa], b_rope[rb]], writes=[b_kt1])
                self.op(DVE, lambda: nc.vector.tensor_tensor(out=kt2[:], in0=pz[pb][0:64, :], in1=rope[rb][:, 1, :], op=ALU.mult),
                        reads=[b_pz[pb], b_rope[rb]], writes=[b_kt2])
                ki = tb % 2
                self.op(DVE, lambda: nc.vector.tensor_tensor(out=krs[ki][:], in0=kt1[:], in1=kt2[:], op=ALU.add),
                        reads=[b_kt1, b_kt2], writes=[b_krs[ki]])
                self.dma(POOL, self.KROPE[:, t0:t0 + 512], krs[ki][:], reads=[b_krs[ki]], writes=[b_dram], is_output=("KROPE" in self.debug))

            blks = list(getattr(self, "a1_blocks", range(NB)))
            nbk = len(blks)
            front(blks[0])
            if nbk > 1:
                front(blks[1])
            backT(blks[0])
            for bi, tb in enumerate(blks):
                if bi + 2 < nbk:
                    front(blks[bi + 2])
                if bi + 1 < nbk:
                    backT(blks[bi + 1])
                backP(tb)
            self.barrier()


    def phase_g(self):
        nc = self.nc
        PE, ACT, DVE, POOL, SP = self.PE, self.ACT, self.DVE, self.POOL, self.SP
        with ExitStack() as es:
            wq = self.sb(es, "g_wq", [128, 16, 512], BF16)
            wk = self.sb(es, "g_wk", [128, 16, 512], BF16)
            wv = self.sb(es, "g_wv", [128, 16, 1024], BF16)
            wa = self.sb(es, "g_wa", [128, 16, 16], BF16)
            wr = self.sb(es, "g_wr", [128, 16, 1024], BF16)
            wg = self.sb(es, "g_wg", [16, 512], BF16)
            keep = self.sb(es, "g_keep", [128, 2, 512], F32)
            mask128 = self.sb(es, "g_mask", [128, 128], F32)
            hT = self.sb(es, "g_hT", [128, 16, 512], BF16)
            vtok = self.sb(es, "g_vtok", [128, 4, 1024], BF16)
            aT = self.sb(es, "g_aT", [16, 512], BF16)
            lsb = self.sb(es, "g_l", [128, 512], F32)
            cs = self.sb(es, "g_cs", [128, 512], F32)
            E1 = self.sb(es, "g_E1", [128, 512], F32)
            E2 = self.sb(es, "g_E2", [128, 512], F32)
            dec = self.sb(es, "g_dec", [128, 8], F32)
            nb1 = self.sb(es, "g_nb1", [128, 2], F32)
            kinv_f = self.sb(es, "g_kinvf", [128, 512], F32)
            kinvT = self.sb(es, "g_kinvT", [128, 512], BF16)
            qdecT = self.sb(es, "g_qdecT", [128, 512], BF16)
            kdecT = self.sb(es, "g_kdecT", [128, 512], BF16)
            kdec_tok = self.sb(es, "g_kdtok", [128, 4, 128], BF16)
            attn_sb = self.sb(es, "g_attn", [128, 128], BF16)
            state = self.sb(es, "g_state", [128, 4, 256], F32)
            state_bf = self.sb(es, "g_statebf", [128, 4, 256], BF16)
            sq = self.sb(es, "g_sq", [128, 2, 512], BF16)
            rstd = self.sb(es, "g_rstd", [128, 512], F32)
            sig = self.sb(es, "g_sig", [128, 512], F32)
            srT = self.sb(es, "g_sr", [128, 2, 512], F32)
            tmp = self.sb(es, "g_tmp", [128, 512], F32)
            stg = [self.sb(es, f"g_stg{i}", [128, 2, 512], BF16) for i in range(2)]
            pl = [self.sb(es, f"g_pl{h}", [128, 512], F32) for h in range(4)]
            pcs = [self.sb(es, f"g_pcs{h}", [128, 512], F32) for h in range(4)]
            pkd = [self.sb(es, f"g_pkd{h}", [128, 512], BF16) for h in range(4)]
            pkt = [self.sb(es, f"g_pkt{h}", [128, 4, 128], BF16) for h in range(4)]
            pnb = self.sb(es, "g_pnb", [128, 8], F32)
            E1t = [E1, self.sb(es, "g_E1b", [128, 512], F32)]
            E2t = [E2, self.sb(es, "g_E2b", [128, 512], F32)]
            kinvf = [kinv_f, self.sb(es, "g_kinvfb", [128, 512], F32)]
            kdt = [kdecT, self.sb(es, "g_kdtb", [128, 512], BF16)]
            qd = [qdecT] + [self.sb(es, f"g_qd{i}", [128, 512], BF16) for i in range(1, 4)]
            attn4 = [attn_sb] + [self.sb(es, f"g_attn{i}", [128, 128], BF16) for i in range(1, 4)]
            dec4 = self.sb(es, "g_dec4", [128, 4, 8], F32)
            rstd2 = [rstd, self.sb(es, "g_rstdb", [128, 512], F32)]
            sig2 = [sig, self.sb(es, "g_sigb", [128, 512], F32)]
            tmp2 = [tmp, self.sb(es, "g_tmpb", [128, 512], F32)]
            b_E1t, b_E2t, b_kinvfL, b_kdt = ([Buf(), Buf()] for _ in range(4))
            b_qd = [Buf() for _ in range(4)]
            b_attn4 = [Buf() for _ in range(4)]
            b_dec4 = [Buf() for _ in range(4)]
            b_rstd2, b_sig2, b_tmp2 = ([Buf(), Buf()] for _ in range(3))
            b_pl = [Buf() for _ in range(4)]
            b_pcs = [Buf() for _ in range(4)]
            b_pkd = [Buf() for _ in range(4)]
            b_pkt = [Buf() for _ in range(4)]
            b_pnb = [Buf() for _ in range(4)]
            banks = [self.bank(es, f"g_bank{i}", BF16 if i == 3 else F32) for i in range(8)]
            bA, bB, bG, bTR, bAT, bDS, bO0, bO1 = banks
            pb = [Buf(psum=True) for _ in range(8)]
            pA, pB, pG, pTR, pAT, pDS, pO0, pO1 = pb
            b_w, b_hT, b_vtok, b_aT, b_l, b_cs, b_E1, b_E2, b_dec, b_nb1 = (Buf() for _ in range(10))
            b_kinvf, b_kinvT, b_qdecT, b_kdecT, b_kdtok, b_attn = (Buf() for _ in range(6))
            b_state = [Buf() for _ in range(4)]
            b_statebf = [Buf() for _ in range(4)]
            b_sq, b_rstd, b_sig, b_sr, b_tmp = (Buf() for _ in range(5))
            b_stg = [Buf(), Buf()]
            b_dram = Buf()
            bc = self.b_const

            wsrc = self.w_in.rearrange("(kc p) n -> p kc n", p=128)
            self.dma(POOL, wk[:], wsrc[:, :, C_KG:C_KG + 512], writes=[b_w])
            self.dma(POOL, wv[:], wsrc[:, :, C_VG:C_VG + 1024], writes=[b_w])
            self.dma(POOL, wa[:], wsrc[:, :, C_AG:C_AG + 16], writes=[b_w])
            self.dma(POOL, wg[:], self.w_gate_up[:, :], writes=[b_w])
            self.dma(POOL, wq[:], wsrc[:, :, C_QG:C_QG + 512], writes=[b_w])
            self.dma(POOL, wr[:], wsrc[:, :, C_RG:C_RG + 1024], writes=[b_w])
            self.dma(SP, keep[:, 0, :], self.c_keep[:, :], writes=[b_w])
            self.dma(SP, mask128[:], self.c_mask128[:, :], writes=[b_w])
            if "FFN" in self.phases:
                self.precast_ffn()
            self.op(DVE, lambda: nc.vector.memset(keep[:, 1, :], 1.0), writes=[b_w])
            for h in range(4):
                self.op(DVE, lambda: nc.vector.memset(state[:, h, :], 0.0), writes=[b_state[h]])
                self.op(DVE, lambda: nc.vector.memset(state_bf[:, h, :], 0.0), writes=[b_statebf[h]])

            def proj(bank, pbuf, wt, c0, m):
                self.pre(PE, reads=[b_hT, b_w], writes=[pbuf])
                for kc in range(16):
                    ins = nc.tensor.matmul(bank[0:m, :], lhsT=wt[:, kc, c0:c0 + m], rhs=hT[:, kc, :], start=(kc == 0), stop=(kc == 15))
                self.post(PE, ins, reads=[b_hT, b_w], writes=[pbuf])

            stg_i = 0
            for tb in getattr(self, "g_blocks", range(NB)):
                own = tb >= NBP
                slot = tb // 4
                t0 = tb * 512
                self.dma(SP, hT[:], self.HT[tb], writes=[b_hT])
                for t in range(4):
                    for half in range(2):
                        bank, pbuf = (bA, pA) if half == 0 else (bB, pB)
                        self.pre(PE, reads=[b_hT, b_w], writes=[pbuf])
                        for kc in range(16):
                            ins = nc.tensor.matmul(bank[:], lhsT=hT[:, kc, t * 128:(t + 1) * 128], rhs=wv[:, kc, half * 512:(half + 1) * 512], start=(kc == 0), stop=(kc == 15))
                        self.post(PE, ins, reads=[b_hT, b_w], writes=[pbuf])
                        self.op(ACT, lambda: nc.scalar.copy(out=vtok[:, t, half * 512:(half + 1) * 512], in_=bank[:]), reads=[pbuf], writes=[b_vtok])
                proj(bB, pB, wa, 0, 16)
                self.op(ACT, lambda: nc.scalar.copy(out=aT[:], in_=bB[0:16, :]), reads=[pB], writes=[b_aT])
                if not own:
                    Kb = [banks[2], banks[4], banks[5], banks[6]]
                    Kp = [pb[2], pb[4], pb[5], pb[6]]
                    for h in range(4):
                        proj(Kb[h], Kp[h], wk, h * 128, 128)
                    for h in range(4):
                        gbk, gpb = (bA, pA) if h % 2 == 0 else (bB, pB)
                        self.op(PE, lambda: nc.tensor.matmul(gbk[:], lhsT=wg[:, h * 128:(h + 1) * 128], rhs=aT[:], start=True, stop=True),
                                reads=[b_w, b_aT], writes=[gpb])
                        self.op(ACT, lambda: nc.scalar.activation(out=pl[h][:], in_=gbk[:], func=AF.Exp, scale=-1.0, bias=self.negb[:, h:h + 1]),
                                reads=[gpb, bc], writes=[b_pl[h]])
                    for h in range(4):
                        self.op(ACT, lambda: nc.scalar.activation(out=pl[h][:], in_=pl[h][:], func=AF.Ln, bias=1.0), reads=[b_pl[h]], writes=[b_pl[h]])
                    for h in range(4):
                        self.op(DVE, lambda: nc.vector.tensor_tensor_scan(out=pcs[h][:], data0=keep[:, 1, :], data1=pl[h][:], initial=0.0, op0=ALU.mult, op1=ALU.add),
                                reads=[b_pl[h], b_w], writes=[b_pcs[h]])
                        self.op(DVE, lambda: nc.vector.tensor_scalar(out=pnb[:, 2 * h:2 * h + 1], in0=pcs[h][:, 511:512], scalar1=-1.0 / 16, scalar2=None, op0=ALU.mult),
                                reads=[b_pcs[h]], writes=[b_pnb[h]])
                    for h in range(4):
                        self.op(ACT, lambda: nc.scalar.activation(out=pcs[h][:], in_=pcs[h][:], func=AF.Exp, scale=1.0 / 16, bias=pnb[:, 2 * h:2 * h + 1]),
                                reads=[b_pcs[h], b_pnb[h]], writes=[b_pcs[h]])
                        self.op(ACT, lambda: nc.scalar.activation(out=pnb[:, 2 * h + 1:2 * h + 2], in_=pnb[:, 2 * h:2 * h + 1], func=AF.Exp), reads=[b_pnb[h]], writes=[b_pnb[h]])
                    for h in range(4):
                        self.op(DVE, lambda: nc.vector.scalar_tensor_tensor(out=pkd[h][:], in0=Kb[h][:], scalar=self.smask[:, 4 + slot:5 + slot], in1=pcs[h][:], op0=ALU.mult, op1=ALU.mult),
                                reads=[Kp[h], b_pcs[h], bc], writes=[b_pkd[h]])
                    for h in range(4):
                        self.pre(PE, reads=[b_pkd[h], bc], writes=[pTR])
                        for t in range(4):
                            ins = nc.tensor.transpose(out=bTR[:, t * 128:(t + 1) * 128], in_=pkd[h][:, t * 128:(t + 1) * 128], identity=self.ident_bf[:])
                        self.post(PE, ins, reads=[b_pkd[h], bc], writes=[pTR])
                        if h % 2 == 0:
                            self.op(ACT, lambda: nc.scalar.copy(out=pkt[h][:].rearrange("p t d -> p (t d)"), in_=bTR[:, 0:512]), reads=[pTR], writes=[b_pkt[h]])
                        else:
                            self.op(DVE, lambda: nc.vector.tensor_copy(out=pkt[h][:].rearrange("p t d -> p (t d)"), in_=bTR[:, 0:512]), reads=[pTR], writes=[b_pkt[h]])
                    for h in range(4):
                        self.pre(PE, reads=[b_pkt[h], b_vtok], writes=[pDS])
                        for t in range(4):
                            ins = nc.tensor.matmul(bDS[:, 0:256], lhsT=pkt[h][:, t, :], rhs=vtok[:, t, h * 256:(h + 1) * 256], start=(t == 0), stop=(t == 3))
                        self.post(PE, ins, reads=[b_pkt[h], b_vtok], writes=[pDS])
                        self.op(DVE, lambda: nc.vector.scalar_tensor_tensor(out=state[:, h, :], in0=state[:, h, :], scalar=pnb[:, 2 * h + 1:2 * h + 2], in1=bDS[:, 0:256], op0=ALU.mult, op1=ALU.add),
                                reads=[b_state[h], b_pnb[h], pDS], writes=[b_state[h]])
                        if tb == NBP - 1 or (tb + 1) not in list(getattr(self, "g_blocks", range(NB))):
                            self.op(ACT, lambda: nc.scalar.copy(out=state_bf[:, h, :], in_=state[:, h, :]), reads=[b_state[h]], writes=[b_statebf[h]])
                    continue
                Qb = [banks[4], banks[5]]
                Qp = [pb[4], pb[5]]
                Kb2 = [bA, bB]
                Kp2 = [pA, pB]
                for h in range(4):
                    a2 = h % 2
                    proj(Kb2[a2], Kp2[a2], wk, h * 128, 128)
                    proj(Qb[a2], Qp[a2], wq, h * 128, 128)
                    self.op(PE, lambda: nc.tensor.matmul(bG[:], lhsT=wg[:, h * 128:(h + 1) * 128], rhs=aT[:], start=True, stop=True),
                            reads=[b_w, b_aT], writes=[pG])
                    self.op(ACT, lambda: nc.scalar.activation(out=pl[h][:], in_=bG[:], func=AF.Exp, scale=-1.0, bias=self.negb[:, h:h + 1]),
                            reads=[pG, bc], writes=[b_pl[h]])
                    self.op(ACT, lambda: nc.scalar.activation(out=pl[h][:], in_=pl[h][:], func=AF.Ln, bias=1.0), reads=[b_pl[h]], writes=[b_pl[h]])
                    self.op(DVE, lambda: nc.vector.tensor_tensor_scan(out=pcs[h][:], data0=keep[:, 0, :], data1=pl[h][:], initial=0.0, op0=ALU.mult, op1=ALU.add),
                            reads=[b_pl[h], b_w], writes=[b_pcs[h]])
                    self.op(ACT, lambda: nc.scalar.activation(out=E1t[a2][:], in_=pcs[h][:], func=AF.Exp, scale=1.0 / 16), reads=[b_pcs[h]], writes=[b_E1t[a2]])
                    self.op(ACT, lambda: nc.scalar.activation(out=E2t[a2][:], in_=pcs[h][:], func=AF.Exp, scale=-1.0 / 16), reads=[b_pcs[h]], writes=[b_E2t[a2]])
                    cs3 = pcs[h][:].rearrange("p (c j) -> p c j", j=64)
                    self.op(ACT, lambda: nc.scalar.activation(out=dec4[:, h, :].rearrange("p (c o) -> p c o", o=1), in_=cs3[:, :, 63:64], func=AF.Exp, scale=-1.0 / 16),
                            reads=[b_pcs[h]], writes=[b_dec4[h]])
                    self.op(DVE, lambda: nc.vector.tensor_tensor(out=kinvf[a2][:], in0=Kb2[a2][:], in1=E1t[a2][:], op=ALU.mult), reads=[Kp2[a2], b_E1t[a2]], writes=[b_kinvfL[a2]])
                    self.op(ACT, lambda: nc.scalar.copy(out=pkd[h][:], in_=kinvf[a2][:]), reads=[b_kinvfL[a2]], writes=[b_pkd[h]])
                    self.op(DVE, lambda: nc.vector.tensor_tensor(out=kdt[a2][:].rearrange("p (c j) -> p c j", j=64), in0=kinvf[a2][:].rearrange("p (c j) -> p c j", j=64),
                                                                in1=dec4[:, h, :].rearrange("p (c o) -> p c o", o=1).to_broadcast([128, 8, 64]), op=ALU.mult),
                            reads=[b_kinvfL[a2], b_dec4[h]], writes=[b_kdt[a2]])
                    self.op(DVE, lambda: nc.vector.scalar_tensor_tensor(out=qd[h][:], in0=Qb[a2][:], scalar=float(128 ** -0.5), in1=E2t[a2][:], op0=ALU.mult, op1=ALU.mult),
                            reads=[Qp[a2], b_E2t[a2]], writes=[b_qd[h]])
                    self.pre(PE, reads=[b_kdt[a2], bc], writes=[pTR])
                    for t in range(4):
                        ins = nc.tensor.transpose(out=bTR[:, t * 128:(t + 1) * 128], in_=kdt[a2][:, t * 128:(t + 1) * 128], identity=self.ident_bf[:])
                    self.post(PE, ins, reads=[b_kdt[a2], bc], writes=[pTR])
                    if h % 2 == 0:
                        self.op(ACT, lambda: nc.scalar.copy(out=pkt[h][:].rearrange("p t d -> p (t d)"), in_=bTR[:, 0:512]), reads=[pTR], writes=[b_pkt[h]])
                    else:
                        self.op(DVE, lambda: nc.vector.tensor_copy(out=pkt[h][:].rearrange("p t d -> p (t d)"), in_=bTR[:, 0:512]), reads=[pTR], writes=[b_pkt[h]])
                Ob4 = [banks[4], banks[5], banks[6], banks[7]]
                Op4 = [pb[4], pb[5], pb[6], pb[7]]
                ATb = [bA, bB]
                ATp = [pA, pB]
                for t in range(4):
                    tc = slice(t * 128, (t + 1) * 128)
                    for h in range(4):
                        a2 = h % 2
                        self.op(PE, lambda: nc.tensor.matmul(ATb[a2][:, 0:128], lhsT=pkd[h][:, tc], rhs=qd[h][:, tc], start=True, stop=True),
                                reads=[b_pkd[h], b_qd[h]], writes=[ATp[a2]])
                        self.op(DVE, lambda: nc.vector.tensor_tensor(out=attn4[h][:], in0=ATb[a2][:, 0:128], in1=mask128[:], op=ALU.mult), reads=[ATp[a2], b_w], writes=[b_attn4[h]])
                    for half in range(2):
                        c = t * 2 + half
                        cc = slice(t * 128 + half * 64, t * 128 + half * 64 + 64)
                        hs = slice(half * 64, half * 64 + 64)
                        for h in range(4):
                            self.pre(PE, reads=[b_vtok, b_attn4[h], b_statebf[h], b_qd[h]], writes=[Op4[h]])
                            for vc in range(2):
                                vs = slice(h * 256 + vc * 128, h * 256 + (vc + 1) * 128)
                                oc = slice(vc * 128 + half * 64, vc * 128 + half * 64 + 64)
                                nc.tensor.matmul(Ob4[h][:, oc], lhsT=vtok[:, t, vs], rhs=attn4[h][:, hs], start=True, stop=False)
                                ins = nc.tensor.matmul(Ob4[h][:, oc], lhsT=state_bf[:, h, vc * 128:(vc + 1) * 128], rhs=qd[h][:, cc], start=False, stop=True)
                            self.post(PE, ins, reads=[b_vtok, b_attn4[h], b_statebf[h], b_qd[h]], writes=[Op4[h]])
                            self.op(PE, lambda: nc.tensor.matmul(bG[:, 0:256], lhsT=pkt[h][hs, t, :], rhs=vtok[hs, t, h * 256:(h + 1) * 256], start=True, stop=True),
                                    reads=[b_pkt[h], b_vtok], writes=[pG])
                            self.op(DVE, lambda: nc.vector.scalar_tensor_tensor(out=state[:, h, :], in0=state[:, h, :], scalar=dec4[:, h, c:c + 1], in1=bG[:, 0:256], op0=ALU.mult, op1=ALU.add),
                                    reads=[b_state[h], b_dec4[h], pG], writes=[b_state[h]])
                            self.op(ACT, lambda: nc.scalar.copy(out=state_bf[:, h, :], in_=state[:, h, :]), reads=[b_state[h]], writes=[b_statebf[h]])
                    for h in range(4):
                        self.op(ACT, lambda: nc.scalar.copy(out=pl[h][:, tc], in_=Ob4[h][:, 0:128]), reads=[Op4[h]], writes=[b_pl[h]])
                        self.op(DVE, lambda: nc.vector.tensor_copy(out=pcs[h][:, tc], in_=Ob4[h][:, 128:256]), reads=[Op4[h]], writes=[b_pcs[h]])
                for h in range(4):
                    a2 = h % 2
                    osb = (pl[h], pcs[h])
                    osbuf = (b_pl[h], b_pcs[h])
                    for vc in range(2):
                        self.op(ACT, lambda: nc.scalar.activation(out=sq[:, vc, :], in_=osb[vc][:], func=AF.Square), reads=[osbuf[vc]], writes=[b_sq])
                    self.pre(PE, reads=[b_sq, bc], writes=[pG])
                    for vc in range(2):
                        ins = nc.tensor.matmul(bG[:], lhsT=self.ones_bf[:], rhs=sq[:, vc, :], start=(vc == 0), stop=(vc == 1))
                    self.post(PE, ins, reads=[b_sq, bc], writes=[pG])
                    self.op(DVE, lambda: nc.vector.tensor_scalar(out=rstd2[a2][:], in0=bG[:], scalar1=1.0 / 256, scalar2=EPS, op0=ALU.mult, op1=ALU.add), reads=[pG], writes=[b_rstd2[a2]])
                    self.rsqrt(rstd2[a2][:], b_rstd2[a2])
                    si = stg_i % 2
                    stg_i += 1
                    for vc in range(2):
                        rbank, rbuf = (bA, pA) if vc == 0 else (bB, pB)
                        proj(rbank, rbuf, wr, h * 256 + vc * 128, 128)
                        self.op(ACT, lambda: nc.scalar.activation(out=sig2[vc][:], in_=rbank[:], func=AF.Exp, scale=-1.0), reads=[rbuf], writes=[b_sig2[vc]])
                        self.op(DVE, lambda: nc.vector.tensor_scalar(out=sig2[vc][:], in0=sig2[vc][:], scalar1=1.0, scalar2=None, op0=ALU.add), reads=[b_sig2[vc]], writes=[b_sig2[vc]])
                        self.op(DVE, lambda: nc.vector.reciprocal(out=sig2[vc][:], in_=sig2[vc][:]), reads=[b_sig2[vc]], writes=[b_sig2[vc]])
                        self.op(DVE, lambda: nc.vector.tensor_tensor(out=srT[:, vc, :], in0=rbank[:], in1=sig2[vc][:], op=ALU.mult), reads=[rbuf, b_sig2[vc]], writes=[b_sr])
                        self.op(DVE, lambda: nc.vector.scalar_tensor_tensor(out=tmp2[vc][:], in0=osb[vc][:], scalar=self.gglaT[:, vc:vc + 1], in1=rstd2[a2][:], op0=ALU.mult, op1=ALU.mult),
                                reads=[osbuf[vc], b_rstd2[a2], bc], writes=[b_tmp2[vc]])
                        self.op(DVE, lambda: nc.vector.tensor_tensor(out=stg[si][:, vc, :], in0=tmp2[vc][:], in1=srT[:, vc, :], op=ALU.mult), reads=[b_tmp2[vc], b_sr], writes=[b_stg[si]])
                    o0 = t0 - NPRE
                    self.dma(POOL, self.MIXT[8 + 2 * h:10 + 2 * h, :, o0:o0 + 512].rearrange("c p n -> p c n"), stg[si][:], reads=[b_stg[si]], writes=[b_dram],
                             is_output=("MIXT" in self.debug))
            self.barrier()


    def phase_att(self):
        nc = self.nc
        PE, ACT, DVE, POOL, SP = self.PE, self.ACT, self.DVE, self.POOL, self.SP
        SC = float(192 ** -0.5)
        with ExitStack() as es:
            ckv = self.sb(es, "t_ckv", [128, 2, NT], BF16)
            krp = self.sb(es, "t_krp", [128, NT], BF16)
            cqn = self.sb(es, "t_cqn", [128, 4, NOWN], BF16)
            wuq = self.sb(es, "t_wuq", [128, 4, 1536], BF16)
            wsw = self.sb(es, "t_wsw", [128, 4, 8, 64], BF16)
            wuk = self.sb(es, "t_wuk", [128, 2, 1024], BF16)
            wuv = self.sb(es, "t_wuv", [128, 2, 1024], BF16)
            rope = self.sb(es, "t_rope", [64, 2, NOWN], F32)
            tri = self.sb(es, "t_tri", [128, 128], BF16)
            KT = self.sb(es, "t_KT", [128, NT], BF16)
            V = self.sb(es, "t_V", [128, 64, 128], BF16)
            qn = self.sb(es, "t_qn", [128, NOWN], BF16)
            qr = self.sb(es, "t_qr", [128, NOWN], BF16)
            t1 = self.sb(es, "t_t1", [64, 512], F32)
            t2 = self.sb(es, "t_t2", [64, 512], F32)
            rl = self.sb(es, "t_rl", [128, 512], F32)
            ost = [self.sb(es, f"t_ost{i}", [128, 512], BF16) for i in range(2)]
            SP_b = [self.ps(es, f"t_spair{i}", [128, 1024], F32) for i in range(3)]
            SP_p = [Buf(psum=True) for _ in range(3)]
            banks = [self.bank(es, f"t_bank{i}") for i in range(2)]
            pbs = [Buf(psum=True) for _ in range(2)]
            O_b, O_p = [banks[0], banks[0]], [pbs[0], pbs[0]]
            L_b, L_p = [banks[1], banks[1]], [pbs[1], pbs[1]]
            PT2 = [self.sb(es, f"t_PT2_{i}", [128, 1024], BF16) for i in range(3)]
            b_PT2 = [Buf() for _ in range(3)]
            PS2 = [self.sb(es, f"t_PS2_{i}", [128, 512], BF16) for i in range(3)]
            b_PS2 = [Buf() for _ in range(3)]
            osb = self.sb(es, "t_osb", [128, 512], F32)
            lsb2 = self.sb(es, "t_lsb2", [128, 512], F32)
            b_osb, b_lsb2 = Buf(), Buf()
            b_in, b_w, b_KT, b_V, b_qn, b_qr, b_t1, b_t2, b_rl = (Buf() for _ in range(9))
            b_ost = [Buf(), Buf()]
            b_dram = Buf()
            bc = self.b_const

            self.dma(SP, ckv[:], self.CKVN.rearrange("c p n -> p c n"), writes=[b_in])
            self.op(DVE, lambda: nc.vector.memset(krp[64:128, :], 0.0), writes=[b_in])
            self.op(DVE, lambda: nc.vector.memset(qr[64:128, :], 0.0), writes=[b_qr])
            self.dma(SP, krp[0:64, :], self.KROPE[:, :], writes=[b_in])
            self.dma(SP, cqn[:], self.CQN.rearrange("c p n -> p c n"), writes=[b_in])
            self.dma(SP, rope[:], self.ROPE[:, :, NPRE:NT].rearrange("k p n -> p k n"), writes=[b_in])
            self.dma(SP, tri[:], self.c_tri_bf[:, :], writes=[b_in])
            self.dma(POOL, wuq[:], self.w_uq.rearrange("(kc p) n -> p kc n", p=128), writes=[b_w])
            wq4 = self.w_uq.rearrange("(kc p) (h c) -> p kc h c", p=128, c=192)
            for kc in range(4):
                self.dma(POOL, wsw[:, kc, :, 0:32], wq4[:, kc, :, 160:192], writes=[b_w])
                self.dma(POOL, wsw[:, kc, :, 32:64], wq4[:, kc, :, 128:160], writes=[b_w])
            self.dma(POOL, wuk[:], self.w_uk.rearrange("(kc p) n -> p kc n", p=128), writes=[b_w])
            self.dma(POOL, wuv[:], self.w_uv.rearrange("(kc p) n -> p kc n", p=128), writes=[b_w])
            self.op(DVE, lambda: nc.vector.tensor_scalar(out=rope[:], in0=rope[:], scalar1=SC, scalar2=None, op0=ALU.mult), reads=[b_in], writes=[b_in])

            ev = [0]

            def evac(out_ap, in_ap, pbuf, wbuf, scale=None):
                ev[0] += 1
                if ev[0] % 2 == 0:
                    if scale is None:
                        self.op(DVE, lambda: nc.vector.tensor_copy(out=out_ap, in_=in_ap), reads=[pbuf], writes=[wbuf])
                    else:
                        self.op(DVE, lambda: nc.vector.tensor_scalar(out=out_ap, in0=in_ap, scalar1=scale, scalar2=None, op0=ALU.mult), reads=[pbuf], writes=[wbuf])
                else:
                    if scale is None:
                        self.op(ACT, lambda: nc.scalar.copy(out=out_ap, in_=in_ap), reads=[pbuf], writes=[wbuf])
                    else:
                        self.op(ACT, lambda: nc.scalar.mul(out=out_ap, in_=in_ap, mul=scale), reads=[pbuf], writes=[wbuf])

            gb = [0]

            def genbank():
                gb[0] += 1
                i = gb[0] % 3
                return (SP_b[i][:, 0:512], SP_p[i])

            qt_n = 0
            ost_i = 0
            for h in getattr(self, "att_heads", range(8)):
                for blk in range(NB):
                    bank, pbuf = genbank()
                    self.pre(PE, reads=[b_in, b_w], writes=[pbuf])
                    for c in range(2):
                        ins = nc.tensor.matmul(bank[:], lhsT=wuk[:, c, h * 128:(h + 1) * 128], rhs=ckv[:, c, blk * 512:(blk + 1) * 512], start=(c == 0), stop=(c == 1))
                    self.post(PE, ins, reads=[b_in, b_w], writes=[pbuf])
                    evac(KT[:, blk * 512:(blk + 1) * 512], bank[:], pbuf, b_KT)
                for g in range(16):
                    bank, pbuf = genbank()
                    self.pre(PE, reads=[b_in, b_w], writes=[pbuf])
                    for j in range(4):
                        tl = g * 4 + j
                        for c in range(2):
                            ins = nc.tensor.matmul(bank[:, j * 128:(j + 1) * 128], lhsT=ckv[:, c, tl * 128:(tl + 1) * 128], rhs=wuv[:, c, h * 128:(h + 1) * 128], start=(c == 0), stop=(c == 1))
                    self.post(PE, ins, reads=[b_in, b_w], writes=[pbuf])
                    evac(V[:, g * 4:(g + 1) * 4, :], bank[:].rearrange("p (j d) -> p j d", d=128), pbuf, b_V)
                for blk in range(4):
                    qc = slice(blk * 512, (blk + 1) * 512)
                    bank, pbuf = genbank()
                    self.pre(PE, reads=[b_in, b_w], writes=[pbuf])
                    for kc in range(4):
                        ins = nc.tensor.matmul(bank[:], lhsT=wuq[:, kc, h * 192:h * 192 + 128], rhs=cqn[:, kc, qc], start=(kc == 0), stop=(kc == 3))
                    self.post(PE, ins, reads=[b_in, b_w], writes=[pbuf])
                    evac(qn[:, qc], bank[:], pbuf, b_qn, scale=SC)
                    bank1, pbuf1 = genbank()
                    self.pre(PE, reads=[b_in, b_w], writes=[pbuf1])
                    for kc in range(4):
                        ins = nc.tensor.matmul(bank1[0:64, :], lhsT=wuq[:, kc, h * 192 + 128:h * 192 + 192], rhs=cqn[:, kc, qc], start=(kc == 0), stop=(kc == 3))
                    self.post(PE, ins, reads=[b_in, b_w], writes=[pbuf1])
                    self.op(DVE, lambda: nc.vector.tensor_tensor(out=t1[:], in0=bank1[0:64, :], in1=rope[:, 0, qc], op=ALU.mult), reads=[pbuf1, b_in], writes=[b_t1])
                    bank2, pbuf2 = genbank()
                    self.pre(PE, reads=[b_in, b_w], writes=[pbuf2])
                    for kc in range(4):
                        ins = nc.tensor.matmul(bank2[0:64, :], lhsT=wsw[:, kc, h, :], rhs=cqn[:, kc, qc], start=(kc == 0), stop=(kc == 3))
                    self.post(PE, ins, reads=[b_in, b_w], writes=[pbuf2])
                    self.op(DVE, lambda: nc.vector.tensor_tensor(out=t2[:], in0=bank2[0:64, :], in1=rope[:, 1, qc], op=ALU.mult), reads=[pbuf2, b_in], writes=[b_t2])
                    self.op(DVE, lambda: nc.vector.tensor_tensor(out=qr[0:64, qc], in0=t1[:], in1=t2[:], op=ALU.add), reads=[b_t1, b_t2], writes=[b_qr])

                for i in getattr(self, "att_qtiles", range(4)):
                    q0 = i * 512
                    oi = qt_n % 2
                    qt_n += 1
                    Ob, Op, Lb, Lp = O_b[oi], O_p[oi], L_b[oi], L_p[oi]
                    groups = [([0, 1], 0, 0, False)]
                    for c in range(0, 4 * i, 2):
                        groups.append(([48 + c, 48 + c + 1], 0, 3, False))
                    for j in range(4):
                        groups.append(([48 + 4 * i + j], 128 * j, 3, True))
                    for c in range(2, 48, 2):
                        groups.append(([c, c + 1], 0, c // 16, False))
                    n = len(groups)

                    def issue_S(g):
                        cl, f0, bcol, dg = groups[g]
                        sp = g % 3
                        self.pre(PE, reads=[b_KT, b_in, b_qn, b_qr], writes=[SP_p[sp]])
                        for idx, kc_ in enumerate(cl):
                            ks = slice(kc_ * 128, (kc_ + 1) * 128)
                            o_ = idx * 512
                            nc.tensor.matmul(SP_b[sp][:, o_ + f0:o_ + 512], lhsT=KT[:, ks], rhs=qn[:, q0 + f0:q0 + 512], start=True, stop=False)
                            ins = nc.tensor.matmul(SP_b[sp][:, o_ + f0:o_ + 512], lhsT=krp[:, ks], rhs=qr[:, q0 + f0:q0 + 512], start=False, stop=True)
                        self.post(PE, ins, reads=[b_KT, b_in, b_qn, b_qr], writes=[SP_p[sp]])

                    issue_S(0)
                    issue_S(1)
                    for g in range(n):
                        cl, f0, bcol, dg = groups[g]
                        sp = g % 3
                        w = 512 * len(cl)
                        if g + 2 < n:
                            issue_S(g + 2)
                        self.op(ACT, lambda: nc.scalar.activation(out=PT2[sp][:, f0:w], in_=SP_b[sp][:, f0:w], func=AF.Exp, bias=self.smask[:, bcol:bcol + 1]),
                                reads=[SP_p[sp], bc], writes=[b_PT2[sp]])
                        if dg:
                            self.op(DVE, lambda: nc.vector.tensor_tensor(out=PT2[sp][:, f0:f0 + 128], in0=PT2[sp][:, f0:f0 + 128], in1=tri[:], op=ALU.mult),
                                    reads=[b_PT2[sp], b_in], writes=[b_PT2[sp]])
                        if len(cl) == 2:
                            self.op(DVE, lambda: nc.vector.tensor_tensor(out=PS2[sp][:], in0=PT2[sp][:, 0:512], in1=PT2[sp][:, 512:1024], op=ALU.add),
                                    reads=[b_PT2[sp]], writes=[b_PS2[sp]])
                        self.pre(PE, reads=[b_V, b_PT2[sp]], writes=[Op])
                        for idx, kc_ in enumerate(cl):
                            o_ = idx * 512
                            first = (g == 0 and idx == 0)
                            last = (g == n - 1 and idx == len(cl) - 1)
                            ins = nc.tensor.matmul(Ob[:, f0:512], lhsT=V[:, kc_, :], rhs=PT2[sp][:, o_ + f0:o_ + 512], start=first, stop=last)
                        self.post(PE, ins, reads=[b_V, b_PT2[sp]], writes=[Op])
                        if len(cl) == 2:
                            self.op(PE, lambda: nc.tensor.matmul(Lb[:], lhsT=self.ones_bf[:], rhs=PS2[sp][:], start=(g == 0), stop=(g == n - 1)),
                                    reads=[b_PS2[sp], bc], writes=[Lp])
                        else:
                            self.op(PE, lambda: nc.tensor.matmul(Lb[:, f0:512], lhsT=self.ones_bf[:], rhs=PT2[sp][:, f0:512], start=False, stop=False),
                                    reads=[b_PT2[sp], bc], writes=[Lp])
                    self.op(ACT, lambda: nc.scalar.copy(out=lsb2[:], in_=Lb[:]), reads=[Lp], writes=[b_lsb2])
                    self.op(ACT, lambda: nc.scalar.copy(out=osb[:], in_=Ob[:]), reads=[Op], writes=[b_osb])
                    self.op(DVE, lambda: nc.vector.reciprocal(out=rl[:], in_=lsb2[:]), reads=[b_lsb2], writes=[b_rl])
                    oi2 = ost_i % 2
                    ost_i += 1
                    self.op(DVE, lambda: nc.vector.tensor_tensor(out=ost[oi2][:], in0=osb[:], in1=rl[:], op=ALU.mult), reads=[b_osb, b_rl], writes=[b_ost[oi2]])
                    self.dma(POOL, self.MIXT[h, :, q0:q0 + 512], ost[oi2][:], reads=[b_ost[oi2]], writes=[b_dram], is_output=("MIXT" in self.debug))
            self.barrier()


    def precast_ffn(self):
        POOL = self.POOL
        self.b_wffn = Buf("wffn")
        for src, dst in ((self.w_ffn_gate, self.WG), (self.w_ffn_up, self.WU), (self.w_ffn_down, self.WD)):
            s3 = src.rearrange("(a p) n -> p a n", p=128)
            d3 = dst.rearrange("(a p) n -> p a n", p=128)
            na = s3.shape[1]
            step = 4
            for a0 in range(0, na, step):
                a1 = min(na, a0 + step)
                self.dma(POOL, d3[:, a0:a1, :], s3[:, a0:a1, :], writes=[self.b_wffn])

    def phase_out(self):
        nc = self.nc
        PE, ACT, DVE, POOL, SP = self.PE, self.ACT, self.DVE, self.POOL, self.SP
        with ExitStack() as es:
            wo = self.sb(es, "o_wo", [128, 16, D], BF16)
            gm = self.sb(es, "o_gm", [128, D], F32)
            mixT = [self.sb(es, f"o_mixT{i}", [128, 16, 512], BF16) for i in range(2)]
            xt = [self.sb(es, f"o_xt{i}", [128, D], F32) for i in range(2)]
            x1 = [self.sb(es, f"o_x1{i}", [128, D], F32) for i in range(2)]
            tmp = self.sb(es, "o_tmp", [128, D], F32)
            sqj = self.sb(es, "o_sqj", [128, D], BF16)
            ss4 = self.sb(es, "o_ss4", [128, 8], F32)
            st = self.sb(es, "o_st", [128, 8], F32)
            xn2 = self.sb(es, "o_xn2", [128, 4, D], BF16)
            h2T = [self.sb(es, f"o_h2T{i}", [128, 16, 512], BF16) for i in range(2)]
            Y = [self.bank(es, f"o_y{i}") for i in range(4)]
            TR = [self.bank(es, f"o_tr{i}", BF16) for i in range(2)]
            pY = [Buf(psum=True) for _ in range(4)]
            pTR = [Buf(psum=True) for _ in range(2)]
            b_wo, b_gm, b_ss4, b_st, b_tmp, b_sqj, b_xn2 = (Buf() for _ in range(7))
            b_mixT = [Buf(), Buf()]
            b_xt = [Buf(), Buf()]
            b_x1 = [Buf(), Buf()]
            b_h2T = [Buf(), Buf()]
            b_dram = Buf()
            bc = self.b_const
            wsrc = self.w_out.rearrange("(kc p) n -> p kc n", p=128)
            for n in range(4):
                self.dma(POOL, wo[:, :, n * 512:(n + 1) * 512], wsrc[:, :, n * 512:(n + 1) * 512], writes=[b_wo])
            self.dma(SP, gm[:], self.GBC[0, :, :], writes=[b_gm])
            gi = 0
            for b in getattr(self, "out_blocks", range(4)):
                mi = b % 2
                c0 = b * 512
                self.dma(SP, mixT[mi][:], self.MIXT[:, :, c0:c0 + 512].rearrange("c p n -> p c n"), writes=[b_mixT[mi]])
                for t in range(4):
                    xi = gi % 2
                    gi += 1
                    r0 = NPRE + c0 + t * 128
                    self.dma(SP, xt[xi][:], self.xall[r0:r0 + 128, :], writes=[b_xt[xi]])
                    for n in range(4):
                        self.pre(PE, reads=[b_mixT[mi], b_wo], writes=[pY[n]])
                        for kc in range(16):
                            ins = nc.tensor.matmul(Y[n][:], lhsT=mixT[mi][:, kc, t * 128:(t + 1) * 128], rhs=wo[:, kc, n * 512:(n + 1) * 512], start=(kc == 0), stop=(kc == 15))
                        self.post(PE, ins, reads=[b_mixT[mi], b_wo], writes=[pY[n]])
                    self.op(DVE, lambda: nc.vector.memset(ss4[:], 0.0), writes=[b_ss4])
                    for n in range(4):
                        self.op(ACT, lambda: nc.scalar.activation(out=sqj[:, n * 512:(n + 1) * 512], in_=Y[n][:], func=AF.Square, accum_out=ss4[:, n:n + 1]),
                                reads=[pY[n], b_ss4], writes=[b_sqj, b_ss4])
                    self.op(DVE, lambda: nc.vector.reduce_sum(out=st[:, 0:1], in_=ss4[:, 0:4], axis=mybir.AxisListType.X), reads=[b_ss4], writes=[b_st])
                    self.op(DVE, lambda: nc.vector.tensor_scalar(out=st[:, 0:1], in0=st[:, 0:1], scalar1=1.0 / D, scalar2=EPS, op0=ALU.mult, op1=ALU.add), reads=[b_st], writes=[b_st])
                    self.rsqrt(st[:, 0:1], b_st)
                    for n in range(4):
                        cs_ = slice(n * 512, (n + 1) * 512)
                        self.op(DVE, lambda: nc.vector.scalar_tensor_tensor(out=tmp[:, cs_], in0=Y[n][:], scalar=st[:, 0:1], in1=gm[:, cs_], op0=ALU.mult, op1=ALU.mult),
                                reads=[pY[n], b_st, b_gm], writes=[b_tmp])
                    self.op(DVE, lambda: nc.vector.tensor_tensor(out=x1[xi][:], in0=tmp[:], in1=xt[xi][:], op=ALU.add), reads=[b_tmp, b_xt[xi]], writes=[b_x1[xi]])
                    self.dma(POOL, self.X1[c0 + t * 128:c0 + (t + 1) * 128, :], x1[xi][:], reads=[b_x1[xi]], writes=[b_dram])
                    self.op(DVE, lambda: nc.vector.memset(ss4[:, 4:5], 0.0), writes=[b_ss4])
                    self.op(ACT, lambda: nc.scalar.activation(out=sqj[:], in_=x1[xi][:], func=AF.Square, accum_out=ss4[:, 4:5]),
                            reads=[b_x1[xi], b_ss4], writes=[b_sqj, b_ss4])
                    self.op(DVE, lambda: nc.vector.tensor_scalar(out=st[:, 1:2], in0=ss4[:, 4:5], scalar1=1.0 / D, scalar2=EPS, op0=ALU.mult, op1=ALU.add), reads=[b_ss4], writes=[b_st])
                    self.rsqrt(st[:, 1:2], b_st)
                    self.op(DVE, lambda: nc.vector.tensor_scalar(out=xn2[:, t, :], in0=x1[xi][:], scalar1=st[:, 1:2], scalar2=None, op0=ALU.mult),
                            reads=[b_x1[xi], b_st], writes=[b_xn2])
                hi = b % 2
                for fc in range(16):
                    pi = fc % 2
                    self.pre(PE, reads=[b_xn2, bc], writes=[pTR[pi]])
                    for t in range(4):
                        ins = nc.tensor.transpose(out=TR[pi][:, t * 128:(t + 1) * 128], in_=xn2[:, t, fc * 128:(fc + 1) * 128], identity=self.ident_bf[:])
                    self.post(PE, ins, reads=[b_xn2, bc], writes=[pTR[pi]])
                    if fc % 2 == 0:
                        self.op(DVE, lambda: nc.vector.tensor_scalar(out=h2T[hi][:, fc, :], in0=TR[pi][:, 0:512], scalar1=self.Af[:, fc:fc + 1], scalar2=self.adaT[:, 48 + fc:49 + fc], op0=ALU.mult, op1=ALU.add),
                                reads=[pTR[pi], self.b_AB, self.b_ada], writes=[b_h2T[hi]])
                    else:
                        self.op(ACT, lambda: nc.scalar.activation(out=h2T[hi][:, fc, :], in_=TR[pi][:, 0:512], func=AF.Identity, scale=self.Af[:, fc:fc + 1], bias=self.adaT[:, 48 + fc:49 + fc]),
                                reads=[pTR[pi], self.b_AB, self.b_ada], writes=[b_h2T[hi]])
                self.dma(POOL, self.H2T[b], h2T[hi][:], reads=[b_h2T[hi]], writes=[b_dram])
            self.barrier()

    def phase_ffn(self):
        nc = self.nc
        PE, ACT, DVE, POOL, SP = self.PE, self.ACT, self.DVE, self.POOL, self.SP
        NJ = D_FF // 128
        with ExitStack() as es:
            h2T = self.sb(es, "f_h2T", [128, 16, 512], BF16)
            aT = self.sb(es, "f_aT", [128, NJ, 512], BF16)
            fT = self.sb(es, "f_fT", [128, 16, 512], F32)
            slab = [self.sb(es, f"f_slab{i}", [128, 4096], BF16) for i in range(4)]
            sg = [self.sb(es, f"f_sg{i}", [128, 512], F32) for i in range(2)]
            sqf = [self.sb(es, f"f_sqf{i}", [128, 512], BF16) for i in range(2)]
            rbc = self.sb(es, "f_rbc", [128, 512], F32)
            x1t = [self.sb(es, f"f_x1t{i}", [128, D], F32) for i in range(2)]
            ot = [self.sb(es, f"f_ot{i}", [128, D], F32) for i in range(2)]
            Fb = [self.bank(es, f"f_F{i}") for i in range(4)]
            SSb = self.bank(es, "f_SS")
            Tb = [self.bank(es, f"f_T{i}") for i in range(3)]
            pF = [Buf(psum=True) for _ in range(4)]
            pSS = Buf(psum=True)
            pT = [Buf(psum=True) for _ in range(3)]
            b_h2T, b_aT, b_fT, b_rbc = (Buf() for _ in range(4))
            b_slab = [Buf() for _ in range(4)]
            b_sg = [Buf(), Buf()]
            b_sqf = [Buf(), Buf()]
            b_x1t = [Buf(), Buf()]
            b_ot = [Buf(), Buf()]
            b_dram = Buf()
            bc = self.b_const
            wg3 = self.WG.rearrange("(kc p) n -> p kc n", p=128)
            wu3 = self.WU.rearrange("(kc p) n -> p kc n", p=128)
            wd3 = self.WD.rearrange("(j p) n -> p j n", p=128)
            sl_i = 0
            ti = 0
            def gateup(b):
                nonlocal sl_i, ti
                c0 = b * 512
                self.dma(SP, h2T[:], self.H2T[b], writes=[b_h2T])
                for g in range(NJ // 2):
                    sgi, sui = sl_i % 4, (sl_i + 1) % 4
                    sl_i += 2
                    wgs = slab[sgi][:].rearrange("p (k n) -> p k n", n=256)
                    wus = slab[sui][:].rearrange("p (k n) -> p k n", n=256)
                    self.dma(SP, wgs, wg3[:, :, g * 256:(g + 1) * 256], writes=[b_slab[sgi]])
                    self.dma(SP, wus, wu3[:, :, g * 256:(g + 1) * 256], writes=[b_slab[sui]])
                    for c in range(2):
                        j = 2 * g + c
                        gb_, ub_ = (0, 1) if j % 2 == 0 else (2, 3)
                        self.pre(PE, reads=[b_h2T, b_slab[sgi]], writes=[pF[gb_]])
                        for kc in range(16):
                            ins = nc.tensor.matmul(Fb[gb_][:], lhsT=wgs[:, kc, c * 128:(c + 1) * 128], rhs=h2T[:, kc, :], start=(kc == 0), stop=(kc == 15))
                        self.post(PE, ins, reads=[b_h2T, b_slab[sgi]], writes=[pF[gb_]])
                        self.pre(PE, reads=[b_h2T, b_slab[sui]], writes=[pF[ub_]])
                        for kc in range(16):
                            ins = nc.tensor.matmul(Fb[ub_][:], lhsT=wus[:, kc, c * 128:(c + 1) * 128], rhs=h2T[:, kc, :], start=(kc == 0), stop=(kc == 15))
                        self.post(PE, ins, reads=[b_h2T, b_slab[sui]], writes=[pF[ub_]])
                        si = j % 2
                        self.op(ACT, lambda: nc.scalar.activation(out=sg[si][:], in_=Fb[gb_][:], func=AF.Silu), reads=[pF[gb_]], writes=[b_sg[si]])
                        self.op(DVE, lambda: nc.vector.tensor_tensor(out=aT[:, j, :], in0=Fb[ub_][:], in1=sg[si][:], op=ALU.mult), reads=[pF[ub_], b_sg[si]], writes=[b_aT])

            def down(b):
                nonlocal sl_i, ti
                c0 = b * 512
                groups = [(j0, min(NJ, j0 + 8)) for j0 in range(0, NJ, 8)]
                for qd in range(4):
                    for (j0, j1) in groups:
                        sdi = sl_i % 4
                        sl_i += 1
                        wds = slab[sdi][:].rearrange("p (j n) -> p j n", n=512)
                        self.dma(SP, wds[:, 0:j1 - j0, :], wd3[:, j0:j1, qd * 512:(qd + 1) * 512], writes=[b_slab[sdi]])
                        self.pre(PE, reads=[b_aT, b_slab[sdi]], writes=pF)
                        for j in range(j0, j1):
                            for dl in range(4):
                                ins = nc.tensor.matmul(Fb[dl][:], lhsT=wds[:, j - j0, dl * 128:(dl + 1) * 128], rhs=aT[:, j, :], start=(j == 0), stop=(j == NJ - 1))
                        self.post(PE, ins, reads=[b_aT, b_slab[sdi]], writes=pF)
                    for dl in range(4):
                        dch = qd * 4 + dl
                        self.op(ACT, lambda: nc.scalar.copy(out=fT[:, dch, :], in_=Fb[dl][:]), reads=[pF[dl]], writes=[b_fT])
                        qi = dch % 2
                        self.op(DVE, lambda: nc.vector.tensor_tensor(out=sqf[qi][:], in0=fT[:, dch, :], in1=fT[:, dch, :], op=ALU.mult), reads=[b_fT], writes=[b_sqf[qi]])
                        self.op(PE, lambda: nc.tensor.matmul(SSb[:], lhsT=self.ones_bf[:], rhs=sqf[qi][:], start=(dch == 0), stop=(dch == 15)),
                                reads=[b_sqf[qi], bc], writes=[pSS])

            def final(b):
                nonlocal sl_i, ti
                c0 = b * 512
                self.op(DVE, lambda: nc.vector.tensor_scalar(out=rbc[:], in0=SSb[:], scalar1=1.0 / D, scalar2=EPS, op0=ALU.mult, op1=ALU.add), reads=[pSS], writes=[b_rbc])
                self.rsqrt(rbc[:], b_rbc)
                for dch in range(16):
                    self.op(DVE, lambda: nc.vector.scalar_tensor_tensor(out=fT[:, dch, :], in0=fT[:, dch, :], scalar=self.gfT[:, dch:dch + 1], in1=rbc[:], op0=ALU.mult, op1=ALU.mult),
                            reads=[b_fT, b_rbc, self.b_AB], writes=[b_fT])
                for t in range(4):
                    xi = ti % 2
                    ti += 1
                    self.dma(SP, x1t[xi][:], self.X1[c0 + t * 128:c0 + (t + 1) * 128, :], writes=[b_x1t[xi]])
                    for n in range(4):
                        tb_ = (t * 4 + n) % 3
                        self.pre(PE, reads=[b_fT, bc], writes=[pT[tb_]])
                        for dl in range(4):
                            ins = nc.tensor.transpose(out=Tb[tb_][:, dl * 128:(dl + 1) * 128], in_=fT[:, n * 4 + dl, t * 128:(t + 1) * 128], identity=self.ident_f[:])
                        self.post(PE, ins, reads=[b_fT, bc], writes=[pT[tb_]])
                        self.op(DVE, lambda: nc.vector.tensor_tensor(out=ot[xi][:, n * 512:(n + 1) * 512], in0=Tb[tb_][:], in1=x1t[xi][:, n * 512:(n + 1) * 512], op=ALU.add),
                                reads=[pT[tb_], b_x1t[xi]], writes=[b_ot[xi]])
                    self.dma(POOL, self.out[c0 + t * 128:c0 + (t + 1) * 128, :], ot[xi][:], reads=[b_ot[xi]], writes=[b_dram], is_output=True)

            fb = list(getattr(self, "ffn_blocks", range(4)))
            gateup(fb[0])
            for bi, b in enumerate(fb):
                down(b)
                if bi + 1 < len(fb):
                    gateup(fb[bi + 1])
                final(b)
            self.barrier()


def _consts():
    c = {}
    c["c_ident_bf"] = np.eye(128, dtype=np.float32).astype(ml_dtypes.bfloat16)
    c["c_ident_f"] = np.eye(128, dtype=np.float32)
    misc = np.zeros((128, 8), np.float32)
    inv_freq = (1.0 / (10000.0 ** (np.arange(0, 64, 2, dtype=np.float32) / np.float32(64)))).astype(np.float32)
    p = np.arange(128)
    misc[:, 0] = inv_freq[p % 32]
    misc[:, 1] = np.where((p % 64) < 32, -1.0, 1.0)
    misc[:, 2] = -0.5
    c["c_misc"] = misc
    keep = np.ones((128, 512), np.float32)
    keep[:, ::64] = 0.0
    c["c_keep"] = keep
    c["c_tri_bf"] = (np.arange(128)[:, None] <= np.arange(128)[None, :]).astype(np.float32).astype(ml_dtypes.bfloat16)
    j = np.arange(128)[:, None]
    cc = np.arange(128)[None, :]
    c["c_mask128"] = ((j // 64 == cc // 64) & (j <= cc)).astype(np.float32)
    return c


def make_in_maps(inputs):
    x = np.asarray(inputs["x"])
    pos = np.asarray(inputs["positions"]).astype(np.int32)
    cc = np.asarray(inputs["c"])
    consts = _consts()
    shared = {}
    for k in ("w_ada", "w_in", "w_uq", "w_uk", "w_uv", "w_gate_up", "w_out", "w_ffn_gate", "w_ffn_up", "w_ffn_down"):
        shared[k] = np.ascontiguousarray(np.asarray(inputs[k])[0])
    for k in ("b_ada", "g_post_mix", "g_post_ffn"):
        shared[k] = np.ascontiguousarray(np.asarray(inputs[k])[0][None, :])
    for k in ("g_pre_mix", "g_q", "g_kv", "b_gate", "g_gla", "g_pre_ffn", "g_post_ffn"):
        v = np.asarray(inputs[k])[0]
        shared[k + "T"] = np.ascontiguousarray(v.reshape(-1, 128).T)
    maps = []
    for core in range(8):
        b, q = core // 4, core % 4
        quarters, valid = [], []
        for s in range(3):
            qs = s - (3 - q)
            valid.append(qs >= 0)
            quarters.append(max(qs, 0))
        quarters.append(q)
        xall = np.concatenate([x[b, 2048 * k:2048 * (k + 1)] for k in quarters], 0)
        posall = np.concatenate([pos[b, 2048 * k:2048 * (k + 1)] for k in quarters], 0)[None, :]
        sm = np.zeros((128, 8), np.float32)
        for s in range(3):
            sm[:, s] = 0.0 if valid[s] else NEG
            sm[:, 4 + s] = 1.0 if valid[s] else 0.0
        m = dict(shared)
        m.update(consts)
        m["xall"] = np.ascontiguousarray(xall)
        m["posall"] = np.ascontiguousarray(posall)
        m["cvec"] = np.ascontiguousarray(cc[b].reshape(16, 128).T)
        m["slotmask"] = sm
        maps.append(m)
    return maps


_CACHE = {}


def kernel(**inputs):
    maps = make_in_maps(inputs)
    if "nc" not in _CACHE:
        _CACHE["nc"] = Prog().build()
    res = run_bass_kernel_spmd(_CACHE["nc"], maps, core_ids=list(range(8)))
    out = np.zeros((2, 8192, D), np.float32)
    for core in range(8):
        b, q = core // 4, core % 4
        out[b, 2048 * q:2048 * (q + 1)] = np.asarray(res.results[core]["out"])
    return out
```
